# Optimizing a Trainium2 kernel written in Bass

```python
import jax, jax.numpy as jnp
from jax import lax
import numpy as np

D_MODEL = 1024
BATCH = 4
SEQ = 8192
DEPTH = 2

D_FF = 2816
SSD_EXPAND = 2
SSD_D_INNER = SSD_EXPAND * D_MODEL
SSD_HEAD_DIM = 64
SSD_HEADS = SSD_D_INNER // SSD_HEAD_DIM
SSD_GROUPS = 4
SSD_HPG = SSD_HEADS // SSD_GROUPS
SSD_STATE = 128
SSD_CONV = 4
SSD_CHUNK = 128
SSD_CONV_DIM = SSD_D_INNER + 2 * SSD_GROUPS * SSD_STATE
MLA_HEADS = 8
MLA_Q_LORA = 512
MLA_KV_LORA = 256
MLA_NOPE = 128
MLA_ROPE = 64
MLA_V = 128
MLA_QK = MLA_NOPE + MLA_ROPE
ATTN_BLOCK = 128
ROPE_THETA = 10000.0
EPS = 1e-6
IN_SPLIT_SIZES = (SSD_D_INNER, SSD_CONV_DIM, SSD_HEADS, MLA_Q_LORA, MLA_KV_LORA, MLA_ROPE, 2 * D_MODEL)
D_IN_PROJ = SSD_D_INNER + SSD_CONV_DIM + SSD_HEADS + MLA_Q_LORA + MLA_KV_LORA + MLA_ROPE + 2 * D_MODEL

kernel_name = "macaron_gated_ssd_mla_hybrid"


def rms_norm(x, g):
    xf = x.astype(jnp.float32)
    y = xf * lax.rsqrt(jnp.mean(xf * xf, axis=-1, keepdims=True) + EPS)
    return (y * g.astype(jnp.float32)).astype(x.dtype)


def swiglu(x, w13, w2):
    gu = x @ w13
    gate, up = gu[..., :D_FF], gu[..., D_FF:]
    return (jax.nn.silu(gate) * up) @ w2


def split_cols(t, sizes):
    out, start = [], 0
    for n in sizes:
        out.append(t[..., start:start + n])
        start += n
    return out


def rope_tables(positions):
    inv = 1.0 / (ROPE_THETA ** (jnp.arange(0, MLA_ROPE, 2, dtype=jnp.float32) / MLA_ROPE))
    ang = positions.astype(jnp.float32)[..., None] * inv
    return jnp.cos(ang), jnp.sin(ang)


def apply_rope(t, cos, sin):
    half = MLA_ROPE // 2
    tf = t.astype(jnp.float32)
    t1, t2 = tf[..., :half], tf[..., half:]
    c, s = cos[:, :, None, :], sin[:, :, None, :]
    return jnp.concatenate([t1 * c - t2 * s, t2 * c + t1 * s], axis=-1).astype(t.dtype)


def causal_depthwise_conv(t, w, b):
    y = lax.conv_general_dilated(
        t, w[:, None, :].astype(t.dtype), window_strides=(1,), padding=[(SSD_CONV - 1, 0)],
        dimension_numbers=('NWC', 'WIO', 'NWC'), feature_group_count=t.shape[-1])
    return y + b.astype(t.dtype)


def ssd_chunked_scan(xh, dt, a, bmat, cmat):
    bsz, s = xh.shape[0], xh.shape[1]
    nc = s // SSD_CHUNK

    def chunks(t):
        t = t.reshape((bsz, nc, SSD_CHUNK) + t.shape[2:])
        return jnp.moveaxis(t, 1, 0)

    x_c = chunks(xh.astype(jnp.float32).reshape(bsz, s, SSD_GROUPS, SSD_HPG, SSD_HEAD_DIM))
    dt_c = chunks(dt.reshape(bsz, s, SSD_GROUPS, SSD_HPG))
    b_c = chunks(bmat.astype(jnp.float32))
    c_c = chunks(cmat.astype(jnp.float32))
    a_g = a.reshape(SSD_GROUPS, SSD_HPG)
    causal = jnp.tril(jnp.ones((SSD_CHUNK, SSD_CHUNK), dtype=bool))[None, :, :, None, None]

    def step(state, inp):
        xc, dtc, bc, cc = inp
        acum = jnp.cumsum(dtc * a_g, axis=1)
        seg = acum[:, :, None] - acum[:, None, :]
        decay = jnp.exp(jnp.where(causal, seg, -jnp.inf))
        cb = jnp.einsum('btgn,bsgn->btsg', cc, bc)
        xdt = xc * dtc[..., None]
        y_diag = jnp.einsum('btsg,btsgh,bsghp->btghp', cb, decay, xdt)
        y_off = jnp.einsum('btgn,bghpn->btghp', cc, state) * jnp.exp(acum)[..., None]
        last = acum[:, -1]
        w_state = jnp.exp(last[:, None] - acum)
        new_state = (state * jnp.exp(last)[..., None, None]
                     + jnp.einsum('bsgn,bsgh,bsghp->bghpn', bc, w_state, xdt))
        return new_state, y_diag + y_off

    state0 = jnp.zeros((bsz, SSD_GROUPS, SSD_HPG, SSD_HEAD_DIM, SSD_STATE), jnp.float32)
    _, y = lax.scan(step, state0, (x_c, dt_c, b_c, c_c))
    return jnp.moveaxis(y, 0, 1).reshape(bsz, s, SSD_HEADS, SSD_HEAD_DIM)


def ssd_branch(z, xbc, dt_raw, conv_w, conv_b, dt_bias, a_log, d_skip, norm_g, w_out):
    bsz, s, _ = z.shape
    xbc = jax.nn.silu(causal_depthwise_conv(xbc, conv_w, conv_b))
    xs, bm, cm = split_cols(xbc, (SSD_D_INNER, SSD_GROUPS * SSD_STATE, SSD_GROUPS * SSD_STATE))
    xh = xs.reshape(bsz, s, SSD_HEADS, SSD_HEAD_DIM)
    bm = bm.reshape(bsz, s, SSD_GROUPS, SSD_STATE)
    cm = cm.reshape(bsz, s, SSD_GROUPS, SSD_STATE)
    dt = jax.nn.softplus(dt_raw.astype(jnp.float32) + dt_bias.astype(jnp.float32))
    a = -jnp.exp(a_log.astype(jnp.float32))
    y = ssd_chunked_scan(xh, dt, a, bm, cm) + xh.astype(jnp.float32) * d_skip.astype(jnp.float32)[:, None]
    y = y.reshape(bsz, s, SSD_D_INNER).astype(z.dtype)
    y = rms_norm(y * jax.nn.silu(z), norm_g)
    return y @ w_out


def causal_block_attention(q, k, v):
    bsz, s, h, dq = q.shape
    nb = s // ATTN_BLOCK
    scale = dq ** -0.5
    qb = jnp.moveaxis(q.reshape(bsz, nb, ATTN_BLOCK, h, dq), 1, 0)
    kpos = jnp.arange(s)

    def one_block(args):
        i, qi = args
        sc = jnp.einsum('bqhd,bkhd->bhqk', qi, k, preferred_element_type=jnp.float32) * scale
        qpos = i * ATTN_BLOCK + jnp.arange(ATTN_BLOCK)
        sc = jnp.where(kpos[None, :] <= qpos[:, None], sc, -jnp.inf)
        p = jax.nn.softmax(sc, axis=-1)
        return jnp.einsum('bhqk,bkhd->bqhd', p.astype(v.dtype), v)

    o = lax.map(one_block, (jnp.arange(nb), qb))
    return jnp.moveaxis(o, 0, 1).reshape(bsz, s, h, v.shape[-1])


def mla_branch(cq, ckv, kr, cos, sin, q_lora_g, w_uq, kv_lora_g, w_ukv, q_norm_g, k_norm_g, w_out):
    bsz, s, _ = cq.shape
    q = (rms_norm(cq, q_lora_g) @ w_uq).reshape(bsz, s, MLA_HEADS, MLA_QK)
    kv = (rms_norm(ckv, kv_lora_g) @ w_ukv).reshape(bsz, s, MLA_HEADS, MLA_NOPE + MLA_V)
    k_nope, v = kv[..., :MLA_NOPE], kv[..., MLA_NOPE:]
    k_pe = jnp.broadcast_to(kr[:, :, None, :], (bsz, s, MLA_HEADS, MLA_ROPE))
    k = jnp.concatenate([k_nope, k_pe], axis=-1)
    q = rms_norm(q, q_norm_g)
    k = rms_norm(k, k_norm_g)
    q = jnp.concatenate([q[..., :MLA_NOPE], apply_rope(q[..., MLA_NOPE:], cos, sin)], axis=-1)
    k = jnp.concatenate([k[..., :MLA_NOPE], apply_rope(k[..., MLA_NOPE:], cos, sin)], axis=-1)
    o = causal_block_attention(q, k, v)
    return o.reshape(bsz, s, MLA_HEADS * MLA_V) @ w_out


def setup_inputs(seed: int = 0) -> dict:
    key = jax.random.key(seed)
    ks = jax.random.split(key, 32)
    f32 = jnp.float32

    def nrm(k, shape, fan_in):
        return jax.random.normal(k, shape, f32) * (fan_in ** -0.5)

    def gain(k, n):
        return 1.0 + 0.02 * jax.random.normal(k, (DEPTH, n), f32)

    dt0 = jnp.exp(jax.random.uniform(ks[10], (DEPTH, SSD_HEADS), f32) * (np.log(0.1) - np.log(0.001)) + np.log(0.001))
    dt_bias = dt0 + jnp.log(-jnp.expm1(-dt0))
    a_log = jnp.log(jax.random.uniform(ks[11], (DEPTH, SSD_HEADS), f32, 1.0, 16.0))
    return {
        "x": jax.random.normal(ks[0], (BATCH, SEQ, D_MODEL), f32),
        "positions": jnp.broadcast_to(jnp.arange(SEQ, dtype=jnp.int32)[None, :], (BATCH, SEQ)),
        "ln_ffn1": gain(ks[1], D_MODEL),
        "ffn1_w13": nrm(ks[2], (DEPTH, D_MODEL, 2 * D_FF), D_MODEL),
        "ffn1_w2": nrm(ks[3], (DEPTH, D_FF, D_MODEL), D_FF),
        "ln_mix": gain(ks[4], D_MODEL),
        "w_in": nrm(ks[5], (DEPTH, D_MODEL, D_IN_PROJ), D_MODEL),
        "conv_w": nrm(ks[6], (DEPTH, SSD_CONV, SSD_CONV_DIM), SSD_CONV),
        "conv_b": 0.02 * jax.random.normal(ks[7], (DEPTH, SSD_CONV_DIM), f32),
        "dt_bias": dt_bias,
        "a_log": a_log,
        "d_skip": 1.0 + 0.1 * jax.random.normal(ks[12], (DEPTH, SSD_HEADS), f32),
        "ssd_norm": gain(ks[13], SSD_D_INNER),
        "w_ssd_out": nrm(ks[14], (DEPTH, SSD_D_INNER, D_MODEL), SSD_D_INNER),
        "q_lora_norm": gain(ks[15], MLA_Q_LORA),
        "w_uq": nrm(ks[16], (DEPTH, MLA_Q_LORA, MLA_HEADS * MLA_QK), MLA_Q_LORA),
        "kv_lora_norm": gain(ks[17], MLA_KV_LORA),
        "w_ukv": nrm(ks[18], (DEPTH, MLA_KV_LORA, MLA_HEADS * (MLA_NOPE + MLA_V)), MLA_KV_LORA),
        "q_norm": gain(ks[19], MLA_QK),
        "k_norm": gain(ks[20], MLA_QK),
        "w_mla_out": nrm(ks[21], (DEPTH, MLA_HEADS * MLA_V, D_MODEL), MLA_HEADS * MLA_V),
        "w_o": nrm(ks[22], (DEPTH, D_MODEL, D_MODEL), D_MODEL),
        "ln_ffn2": gain(ks[23], D_MODEL),
        "ffn2_w13": nrm(ks[24], (DEPTH, D_MODEL, 2 * D_FF), D_MODEL),
        "ffn2_w2": nrm(ks[25], (DEPTH, D_FF, D_MODEL), D_FF),
    }


def reference(x, positions, ln_ffn1, ffn1_w13, ffn1_w2, ln_mix, w_in, conv_w, conv_b, dt_bias,
              a_log, d_skip, ssd_norm, w_ssd_out, q_lora_norm, w_uq, kv_lora_norm, w_ukv,
              q_norm, k_norm, w_mla_out, w_o, ln_ffn2, ffn2_w13, ffn2_w2):
    cos, sin = rope_tables(positions)
    h = x
    for l in range(DEPTH):
        h = h + 0.5 * swiglu(rms_norm(h, ln_ffn1[l]), ffn1_w13[l], ffn1_w2[l])
        u = rms_norm(h, ln_mix[l])
        z, xbc, dt_raw, cq, ckv, kr, gates = split_cols(u @ w_in[l], IN_SPLIT_SIZES)
        y_ssd = ssd_branch(z, xbc, dt_raw, conv_w[l], conv_b[l], dt_bias[l], a_log[l], d_skip[l],
                           ssd_norm[l], w_ssd_out[l])
        y_mla = mla_branch(cq, ckv, kr, cos, sin, q_lora_norm[l], w_uq[l], kv_lora_norm[l], w_ukv[l],
                           q_norm[l], k_norm[l], w_mla_out[l])
        g = jax.nn.sigmoid(gates.astype(jnp.float32)).astype(h.dtype)
        merged = g[..., :D_MODEL] * y_ssd + g[..., D_MODEL:] * y_mla
        h = h + merged @ w_o[l]
        h = h + 0.5 * swiglu(rms_norm(h, ln_ffn2[l]), ffn2_w13[l], ffn2_w2[l])
    return h
```

```python
import math
import time
from contextlib import ExitStack

import numpy as np
import concourse.bass as bass
import concourse.mybir as mybir
from concourse.bass_utils import run_bass_kernel_spmd

dt = mybir.dt
F32, BF16, I32 = dt.float32, dt.bfloat16, dt.int32
AF = mybir.ActivationFunctionType
ALU = mybir.AluOpType

SAME_ENG_SYNC = True
import os as _os
PHASES = _os.environ.get("KPH", "f1,nm,mla,ssd,f2").split(",")
KMLA = int(_os.environ.get("KMLA", "99"))
KSSD = int(_os.environ.get("KSSD", "999999"))
KV = int(_os.environ.get("KV", "99"))


class Tk:
    __slots__ = ("name", "lw", "rd", "sem", "dcnt")

    def __init__(self, name):
        self.name = name
        self.lw = None
        self.rd = []
        self.sem = None
        self.dcnt = 0


class Op:
    __slots__ = ("eng", "fn", "R", "W", "dma", "deps", "sig", "sem", "cnt", "name", "grp", "semtok")


class Prog:
    ENGS = ["pe", "act", "dve", "pool", "sp"]

    def __init__(self, nc, stack):
        self.nc = nc
        self.stack = stack
        self.ops = []
        self.inserts = {}
        self.nsem = 0

    def sbuf(self, name, shape, dtype):
        return self.stack.enter_context(self.nc.sbuf_tensor(name, list(shape), dtype))

    def psum(self, name, shape, dtype):
        return self.stack.enter_context(self.nc.psum_tensor(name, list(shape), dtype))

    def sem(self, name):
        self.nsem += 1
        return self.stack.enter_context(self.nc.semaphore(name))

    def mkop(self, eng, fn, R=(), W=(), dma=False, name="", semtok=None, grp=None):
        op = Op()
        op.eng, op.fn, op.R, op.W, op.dma, op.name = eng, fn, tuple(R), tuple(W), dma, name
        op.sig = False
        op.sem = None
        op.cnt = 0
        op.grp = grp
        op.semtok = semtok
        return op

    limit = None

    def add(self, *a, **k):
        op = self.mkop(*a, **k)
        if self.limit is not None and len(self.ops) >= self.limit:
            return op
        self.ops.append(op)
        return op

    def insert_before(self, pos, op):
        self.inserts.setdefault(pos, []).append(op)

    def finalize(self):
        final = []
        for i, op in enumerate(self.ops):
            if i in self.inserts:
                final.extend(self.inserts[i])
            final.append(op)
        if len(self.ops) in self.inserts:
            final.extend(self.inserts[len(self.ops)])
        self.final = final
        for op in final:
            deps = {}
            for t in op.R:
                if t.lw is not None:
                    deps[id(t.lw)] = t.lw
            for t in op.W:
                if t.lw is not None:
                    deps[id(t.lw)] = t.lw
                for x in t.rd:
                    deps[id(x)] = x
            deps.pop(id(op), None)
            op.deps = [d for d in deps.values() if not (op.grp is not None and d.grp == op.grp)]
            for t in op.R:
                if op.dma:
                    t.rd.append(op)
                else:
                    t.rd = [x for x in t.rd if x.dma or x.eng != op.eng]
                    t.rd.append(op)
            for t in op.W:
                t.lw = op
                t.rd = []
        for op in final:
            if op.dma:
                tok = op.semtok if op.semtok is not None else (op.W[0] if op.W else op.R[0])
                if tok.sem is None:
                    tok.sem = self.sem("d_" + tok.name)
                tok.dcnt += 16
                op.sem = tok.sem
                op.cnt = tok.dcnt
        gmax = {}
        for op in final:
            if op.dma and op.grp is not None:
                k = (op.grp, id(op.sem))
                gmax[k] = max(gmax.get(k, 0), op.cnt)
        for op in final:
            if op.dma and op.grp is not None:
                op.cnt = gmax[(op.grp, id(op.sem))]
        for op in final:
            for d in op.deps:
                if d.dma:
                    continue
                if d.eng != op.eng:
                    d.sig = True
                elif SAME_ENG_SYNC and op.eng != "pe" and not op.dma:
                    d.sig = True
        self.esem = {e: self.sem("e_" + e) for e in self.ENGS}
        ecnt = {e: 0 for e in self.ENGS}
        for op in final:
            if not op.dma and op.sig:
                ecnt[op.eng] += 1
                op.sem = self.esem[op.eng]
                op.cnt = ecnt[op.eng]
        self.by_eng = {e: [op for op in final if op.eng == e] for e in self.ENGS}

    def _run(self, ename, e):
        known = {}
        for op in self.by_eng[ename]:
            waits = {}
            for d in op.deps:
                if not d.dma:
                    if d.eng == ename and (ename == "pe" or not SAME_ENG_SYNC or op.dma):
                        continue
                    if not d.sig:
                        continue
                k = id(d.sem)
                if k not in waits or waits[k][1] < d.cnt:
                    waits[k] = (d.sem, d.cnt)
            for k, (s, v) in waits.items():
                if known.get(k, 0) >= v:
                    continue
                known[k] = v
                e.wait_ge(s, v)
            if op.fn is None:
                continue
            ins = op.fn(e)
            if op.dma:
                ins.then_inc(op.sem, 16)
            elif op.sig:
                ins.then_inc(op.sem, 1)

    def emit(self):
        with self.nc.Block() as blk:
            @blk.tensor
            def _(e):
                self._run("pe", e)

            @blk.scalar
            def _(e):
                self._run("act", e)

            @blk.vector
            def _(e):
                self._run("dve", e)

            @blk.gpsimd
            def _(e):
                self._run("pool", e)

            @blk.sync
            def _(e):
                self._run("sp", e)


class WStream:
    def __init__(self, P, name, shape, dtype, nslots, la, eng="sp"):
        self.P = P
        self.name = name
        self.n = nslots
        self.la = la
        self.eng = eng
        self.buf = [P.sbuf(f"{name}{i}", shape, dtype) for i in range(nslots)]
        self.tok = [Tk(f"{name}{i}") for i in range(nslots)]
        self.reqs = []
        self.minpos = []

    def get(self, pieces, extraR=(), minpos=0):
        if self.P.limit is not None and len(self.P.ops) >= self.P.limit:
            return self.buf[0], self.tok[0]
        n = len(self.reqs)
        self.minpos.append(minpos)
        self.reqs.append((len(self.P.ops), pieces, tuple(extraR)))
        s = n % self.n
        return self.buf[s], self.tok[s]

    def flush(self):
        P = self.P
        import bisect
        tid = {id(t): i for i, t in enumerate(self.tok)}
        reads = [[] for _ in self.tok]
        for pos, op in enumerate(P.ops):
            for t in op.R:
                i = tid.get(id(t))
                if i is not None:
                    reads[i].append(pos)
        nreq = len(self.reqs)
        last_use = [0] * nreq
        for n in range(nreq):
            s = n % self.n
            lo = self.reqs[n][0]
            hi = self.reqs[n + self.n][0] if n + self.n < nreq else len(P.ops) + 1
            r = reads[s]
            a = bisect.bisect_left(r, lo)
            b = bisect.bisect_left(r, hi)
            last_use[n] = r[b - 1] if b > a else lo
        for n, (pos, pieces, extraR) in enumerate(self.reqs):
            s = n % self.n
            ipos = self.reqs[max(0, n - self.la)][0]
            if n - self.n >= 0:
                ipos = max(ipos, last_use[n - self.n] + 1)
            ipos = max(ipos, self.minpos[n])
            assert ipos <= pos, (self.name, n, ipos, pos)
            for (dstf, src) in pieces:
                dst = dstf(self.buf[s])
                op = P.mkop(self.eng, (lambda e, d=dst, s_=src: e.dma_start(out=d, in_=s_)),
                            R=extraR, W=[self.tok[s]], dma=True, name=f"ld_{self.name}{n}",
                            semtok=self.tok[s], grp=(self.name, n))
                P.insert_before(ipos, op)


class V:
    __slots__ = ("ap", "tk")

    def __init__(self, ap, tk):
        self.ap = ap
        self.tk = list(tk)


class Arena:
    GR = 1024

    def __init__(self, P, name, nbytes):
        self.nbytes = nbytes
        self.t = P.sbuf(name, [128, nbytes // 4], F32)
        self.tb = self.t.bitcast(BF16)
        self.ti = self.t.bitcast(I32)
        self.g = [Tk(f"{name}_g{i}") for i in range((nbytes + self.GR - 1) // self.GR)]

    def toks(self, lo, hi):
        return self.g[lo // self.GR:(hi + self.GR - 1) // self.GR]


class Buf:
    def __init__(self, P, name, C, T, dtype, arena=None, off=None):
        esz = 2 if dtype == BF16 else 4
        self.C, self.T, self.cb = C, T, T * esz
        self.arena = arena
        self.off = off
        if arena is None:
            self.t = P.sbuf(name, [128, C, T], dtype)
            self.ap = self.t[:]
            self.own = [Tk(f"{name}_{c}") for c in range(C)]
        else:
            assert off % 4 == 0 and off + C * T * esz <= arena.nbytes, (name, off, C, T)
            base = arena.tb if dtype == BF16 else (arena.ti if dtype == I32 else arena.t)
            e0 = off // esz
            self.ap = base[:, e0:e0 + C * T].rearrange("p (c t) -> p c t", t=T)

    def toks(self, c0, c1=None):
        c1 = c0 + 1 if c1 is None else c1
        if self.arena is None:
            return self.own[c0:c1]
        return self.arena.toks(self.off + c0 * self.cb, self.off + c1 * self.cb)

    def c(self, c, lo=None, hi=None, p0=0, p1=128):
        return V(self.ap[p0:p1, c, lo:hi], self.toks(c))

    def r(self, c0, c1, p0=0, p1=128):
        return V(self.ap[p0:p1, c0:c1, :], self.toks(c0, c1))

    def all(self):
        return self.r(0, self.C)


D = 1024
DFF = 2816
T = 512
NH = 32
HD = 64
NG = 4
NS = 128
DIN = 2048
MH = 8
QL = 512
KVL = 256
EPS = 1e-6
WIN_B = 8096
V_LN1, V_LNM, V_LN2, V_CW, V_CB, V_QLN, V_KVLN = 0, 8, 16, 24, 120, 144, 148
V_QN, V_QR, V_QRS, V_KN, V_KR, V_KRS, V_SSDN, NV = 150, 151, 152, 153, 154, 155, 156, 172
C_ID, C_U, C_INV, NCST = 0, 128, 256, 257
PI = 3.1415925
TWO_PI = 6.283185307179586
CW1 = 6.28125
CW2 = TWO_PI - 6.28125


def build(SEQ, DEPTH, dbg=None):
    NT = SEQ // T
    nc = bass.Bass("TRN2", target_bir_lowering=False)
    inp = lambda n, s, d=F32: nc.dram_tensor(n, list(s), d, kind="ExternalInput").ap()
    scr = lambda n, s, d=BF16: nc.dram_tensor(n, list(s), d, kind="Internal").ap()
    xT = inp("xT", [D, SEQ])
    pos = inp("pos", [1, SEQ], I32)
    cst = inp("cst", [128, NCST])
    oT = nc.dram_tensor("oT", [D, SEQ], F32, kind="ExternalOutput").ap()
    W = []
    for l in range(DEPTH):
        w = {}
        w["vecs"] = inp(f"vecs{l}", [128, NV])
        w["rows"] = inp(f"rows{l}", [1, 96])
        for nm, sh in [("ffn1_w13", [D, 2 * DFF]), ("ffn1_w2", [DFF, D]), ("w_in", [D, 8032]),
                       ("w_ssd_out", [DIN, D]), ("w_uq", [QL, 1536]), ("w_ukv", [KVL, 2048]),
                       ("w_mla_out", [D, D]), ("w_o", [D, D]), ("ffn2_w13", [D, 2 * DFF]), ("ffn2_w2", [DFF, D])]:
            w[nm] = inp(f"{nm}{l}", sh)
        for nm, sh in [("ffn1_w13", [D, 2 * DFF]), ("ffn1_w2", [DFF, D]), ("w_in", [D, WIN_B]),
                       ("w_ssd_out", [DIN, D]), ("w_uq", [QL, 2048]), ("w_ukv", [KVL, 2048]),
                       ("w_mla_out", [D, D]), ("w_o", [D, D]), ("ffn2_w13", [D, 2 * DFF]), ("ffn2_w2", [DFF, D])]:
            w[nm + "_b"] = scr(f"{nm}{l}_b", sh)
            w[nm + "_tk"] = Tk(f"{nm}{l}_b")
        W.append(w)
    hbuf = scr("hbuf", [D, SEQ], F32)
    KN = scr("KN", [MH, NT, 128, T])
    KR = scr("KR", [NT, 64, T])
    VV = scr("VV", [MH, NT, 128, 4, 128])
    dbg_out = None
    if dbg:
        dbg_out = nc.dram_tensor("dbg", [128, dbg], F32, kind="ExternalOutput").ap()

    st = ExitStack()
    with st:
        P = Prog(nc, st)

        def MM(o, l, r, start=True, stop=True, extraR=()):
            P.add("pe", lambda e: e.matmul(o.ap, lhsT=l.ap, rhs=r.ap, start=start, stop=stop),
                  R=l.tk + r.tk + list(extraR), W=o.tk, name="mm")

        def TR(o, i, ident):
            P.add("pe", lambda e: e.transpose(o.ap, i.ap, ident.ap), R=i.tk + ident.tk, W=o.tk)

        def ACT(o, i, func, bias=None, scale=None, accum=None, extraR=()):
            kw = {}
            R = i.tk + list(extraR)
            Wt = list(o.tk)
            if bias is not None:
                if isinstance(bias, V):
                    kw["bias"] = bias.ap
                    R += bias.tk
                else:
                    kw["bias"] = bias
            if scale is not None:
                if isinstance(scale, V):
                    kw["scale"] = scale.ap
                    R += scale.tk
                else:
                    kw["scale"] = scale
            if accum is not None:
                kw["accum_out"] = accum.ap
                Wt += accum.tk
            P.add("act", lambda e: e.activation(out=o.ap, in_=i.ap, func=func, **kw), R=R, W=Wt, name="act_" + str(func).split(".")[-1])

        def TT(eng, o, a, b, op):
            P.add(eng, lambda e: e.tensor_tensor(out=o.ap, in0=a.ap, in1=b.ap, op=op), R=a.tk + b.tk, W=o.tk)

        def TS(eng, o, a, s1, op0, s2=None, op1=None):
            R = list(a.tk)
            k1 = s1
            if isinstance(s1, V):
                R += s1.tk
                k1 = s1.ap
            k2 = s2
            if isinstance(s2, V):
                R += s2.tk
                k2 = s2.ap
            if op1 is None:
                P.add(eng, lambda e: e.tensor_scalar(out=o.ap, in0=a.ap, scalar1=k1, scalar2=None, op0=op0), R=R, W=o.tk)
            else:
                P.add(eng, lambda e: e.tensor_scalar(out=o.ap, in0=a.ap, scalar1=k1, scalar2=k2, op0=op0, op1=op1), R=R, W=o.tk)

        def STT(o, a, s, b, op0, op1):
            R = a.tk + b.tk
            k = s
            if isinstance(s, V):
                R += s.tk
                k = s.ap
            P.add("dve", lambda e: e.scalar_tensor_tensor(out=o.ap, in0=a.ap, scalar=k, in1=b.ap, op0=op0, op1=op1), R=R, W=o.tk)

        def CP(eng, o, i):
            if eng == "act":
                P.add("act", lambda e: e.activation(out=o.ap, in_=i.ap, func=AF.Copy), R=i.tk, W=o.tk, name="act_cp")
            else:
                P.add(eng, lambda e: e.tensor_copy(out=o.ap, in_=i.ap), R=i.tk, W=o.tk)

        def RCP(o, i):
            P.add("dve", lambda e: e.reciprocal(out=o.ap, in_=i.ap), R=i.tk, W=o.tk)

        def MEMSET(eng, o, val):
            P.add(eng, lambda e: e.memset(o.ap, val), W=o.tk)

        def DMA(eng, out_ap, in_ap, R=(), W=(), semtok=None, grp=None):
            return P.add(eng, lambda e: e.dma_start(out=out_ap, in_=in_ap), R=R, W=W, dma=True, semtok=semtok, grp=grp)

        def conv_w(src, dst, tk, pieces, rows, grp):
            for r0 in range(0, rows, 128):
                for (dc, sc, n) in pieces:
                    DMA("pool", dst[r0:r0 + 128, dc:dc + n], src[r0:r0 + 128, sc:sc + n], W=[tk], semtok=tk, grp=grp)

        win_pieces = [(0, 0, 5120), (5120, 5152, 512), (5632, 5664, 256), (5888, 5920, 64), (5952, 5952, 32),
                      (5984, 5920, 32), (6016, 5120, 32), (6048, 5984, 2048)]
        uq_pieces = []
        for h in range(MH):
            uq_pieces += [(128 * h, 192 * h, 128), (1024 + 64 * h, 192 * h + 128, 64),
                          (1536 + 64 * h, 192 * h + 160, 32), (1536 + 64 * h + 32, 192 * h + 128, 32)]
        ukv_pieces = []
        for h in range(MH):
            ukv_pieces += [(128 * h, 256 * h, 128), (1024 + 128 * h, 256 * h + 128, 128)]
        for l in range(DEPTH):
            w = W[l]
            order = [("ffn1_w13", [(0, 0, 2 * DFF)], D), ("ffn1_w2", [(0, 0, D)], DFF), ("w_in", win_pieces, D),
                     ("w_uq", uq_pieces, QL), ("w_ukv", ukv_pieces, KVL), ("w_mla_out", [(0, 0, D)], D),
                     ("w_ssd_out", [(0, 0, D)], DIN), ("w_o", [(0, 0, D)], D),
                     ("ffn2_w13", [(0, 0, 2 * DFF)], D), ("ffn2_w2", [(0, 0, D)], DFF)]
            for nm, pcs, rows in order:
                conv_w(w[nm], w[nm + "_b"], w[nm + "_tk"], pcs, rows, grp=f"cv_{nm}{l}")

        cstb = Buf(P, "cstb", 1, NCST, F32)
        DMA("sp", cstb.ap[:, 0, :], cst, W=cstb.toks(0), semtok=cstb.toks(0)[0])
        identb = Buf(P, "identb", 1, 128, BF16)
        Ub = Buf(P, "Ub", 1, 128, BF16)
        onesb = Buf(P, "onesb", 1, 128, BF16)
        onesf = Buf(P, "onesf", 1, 128, F32)
        ident_f = V(cstb.ap[:, 0, C_ID:C_ID + 128], cstb.toks(0))
        U_f = V(cstb.ap[:, 0, C_U:C_U + 128], cstb.toks(0))
        invf = V(cstb.ap[0:64, 0, C_INV:C_INV + 1], cstb.toks(0))
        CP("dve", identb.c(0), ident_f)
        CP("dve", Ub.c(0), U_f)
        MEMSET("dve", onesb.c(0), 1.0)
        MEMSET("dve", onesf.c(0), 1.0)
        ident_b = identb.c(0)
        epsb = Buf(P, "epsb", 1, 2, F32)
        MEMSET("dve", epsb.c(0), EPS)
        epsc = V(epsb.ap[:, 0, 0:1], epsb.toks(0))
        negm = Buf(P, "negm", 1, 512, BF16)
        for q_ in range(4):
            TS("dve", negm.c(0, q_ * 128, (q_ + 1) * 128), U_f, -1.0, ALU.add, 30000.0, ALU.mult)
        U_b = Ub.c(0)
        ones_b = onesb.c(0)
        ones_f = onesf.c(0)

        hT = Buf(P, "hT", 8, T, F32)
        xn = Buf(P, "xn", 8, T, BF16)
        mg = Buf(P, "mg", 8, T, BF16)
        Sst = Buf(P, "Sst", NG, 512, F32)
        Sb = Buf(P, "Sb", NG, 512, BF16)
        sck = Buf(P, "sck", NT * 4, 8, F32)
        halo = Buf(P, "halo", 24, 3, F32)
        vec = [Buf(P, f"vec{l}", 1, NV, F32) for l in range(DEPTH)]
        rowsb = [Buf(P, f"rowsb{l}", 1, 96, F32) for l in range(DEPTH)]
        aneg = [Buf(P, f"aneg{l}", 1, 32, F32) for l in range(DEPTH)]
        for l in range(DEPTH):
            DMA("sp", vec[l].ap[:, 0, :], W[l]["vecs"], W=vec[l].toks(0), semtok=vec[l].toks(0)[0])
            DMA("sp", rowsb[l].ap[:, 0, :], W[l]["rows"][0, :].partition_broadcast(128), W=rowsb[l].toks(0), semtok=rowsb[l].toks(0)[0])
            ACT(aneg[l].c(0), rowsb[l].c(0, 32, 64), AF.Exp)
            TS("dve", aneg[l].c(0), aneg[l].c(0), -1.0, ALU.mult)
        ws = WStream(P, "ws", [128, 8, 512], BF16, 4, 2)
        kvs = WStream(P, "kvs", [128, 1536], BF16, 4, 3)
        ps = [P.psum(f"ps{i}", [128, 512], F32) for i in range(8)]
        psb = [p.bitcast(BF16) for p in ps]
        tk_ps = [Tk(f"ps{i}") for i in range(8)]

        def PS(i, lo=0, hi=512, p0=0, p1=128):
            return V(ps[i][p0:p1, lo:hi], [tk_ps[i]])

        def PSB(i, lo=0, hi=1024, p0=0, p1=128):
            return V(psb[i][p0:p1, lo:hi], [tk_ps[i]])

        AR = Arena(P, "arena", 110 * 1024)
        KB = 1024
        sq = Buf(P, "sq", 2, T, BF16, AR, 0)
        rs = Buf(P, "rs", 1, T, F32, AR, 2 * KB)
        rstd = Buf(P, "rstd", 1, T, F32, AR, 4 * KB)
        sg = Buf(P, "sg", 2, T, F32, AR, 6 * KB)
        B0 = 10 * KB
        act = Buf(P, "act", 22, T, BF16, AR, B0)
        o = B0
        cqn = Buf(P, "cqn", 4, T, BF16, AR, o); o += 4 * KB
        ckvn = Buf(P, "ckvn", 2, T, BF16, AR, o); o += 2 * KB
        Qn = Buf(P, "Qn", 8, T, BF16, AR, o); o += 8 * KB
        Qr = Buf(P, "Qr", 8, T, BF16, AR, o); o += 8 * KB
        KTs = Buf(P, "KTs", 8, T, BF16, AR, o); o += 8 * KB
        KrT = Buf(P, "KrT", 1, T, BF16, AR, o); o += 1 * KB
        Vs = Buf(P, "Vs", 4, 1024, BF16, AR, o); o += 8 * KB
        PT = Buf(P, "PT", 3, T, BF16, AR, o); o += 3 * KB
        oTb = Buf(P, "oTb", 8, T, BF16, AR, o); o += 8 * KB
        cos2 = Buf(P, "cos2", 1, T, F32, AR, o); o += 2 * KB
        sinS = Buf(P, "sinS", 1, T, F32, AR, o); o += 2 * KB
        ra = Buf(P, "ra", 1, T, F32, AR, o); o += 2 * KB
        rb = Buf(P, "rb", 1, T, F32, AR, o); o += 2 * KB
        rstdq = Buf(P, "rstdq", 1, T, F32, AR, o); o += 2 * KB
        rd = Buf(P, "rd", 2, T, F32, AR, o); o += 4 * KB
        posi = Buf(P, "posi", 1, T, I32, AR, o); o += 2 * KB
        ksq = Buf(P, "ksq", 1, 128, F32, AR, o); o += 1 * KB
        mla_end = o
        o = B0
        zs = Buf(P, "zs", 4, DIN, BF16, AR, o); o += 16 * KB
        dAU = Buf(P, "dAU", 2, 1024, F32, AR, o)
        stg = Buf(P, "stg", 2, 516, F32, AR, o); o += 5 * KB
        cacc = Buf(P, "cacc", 2, T, F32, AR, o); o += 4 * KB
        xsT = Buf(P, "xsT", 16, T, BF16, AR, o); o += 16 * KB
        BT = Buf(P, "BT", 4, T, BF16, AR, o); o += 4 * KB
        CT = Buf(P, "CT", 4, T, BF16, AR, o); o += 4 * KB
        xdt = Buf(P, "xdt", 1, DIN, BF16, AR, o); o += 4 * KB
        xw = Buf(P, "xw", 1, DIN, BF16, AR, o); o += 4 * KB
        xtok = Buf(P, "xtok", 1, DIN, BF16, AR, o); o += 4 * KB
        Btok = Buf(P, "Btok", 1, 512, BF16, AR, o); o += 1 * KB
        Eb = Buf(P, "Eb", 2, 1024, BF16, AR, o); o += 4 * KB
        Mt = Buf(P, "Mt", 2, 1024, BF16, AR, o); o += 4 * KB
        CBm = Buf(P, "CBm", 1, 512, F32, AR, o); o += 2 * KB
        yb = Buf(P, "yb", 1, DIN, F32, AR, o); o += 8 * KB
        ynb = Buf(P, "ynb", 1, DIN, BF16, AR, o); o += 4 * KB
        tmpa = Buf(P, "tmpa", 2, 512, F32, AR, o); o += 4 * KB
        tmpb = Buf(P, "tmpb", 2, 512, F32, AR, o); o += 4 * KB
        dts = Buf(P, "dts", 8, 128, F32, AR, o); o += 4 * KB
        ksq2 = Buf(P, "ksq2", 1, 8, F32, AR, o); o += 1 * KB
        ssd_end = o
        assert mla_end <= AR.nbytes and ssd_end <= AR.nbytes, (mla_end, ssd_end)
        ynT = xsT

        xTv = xT.rearrange("(c p) t -> p c t", p=128)
        oTv = oT.rearrange("(c p) t -> p c t", p=128)
        hbv = hbuf.rearrange("(c p) t -> p c t", p=128)
        tk_hb = [Tk(f"hb{i}") for i in range(NT)]
        tk_kvc = [Tk(f"kvc{i}") for i in range(NT)]
        kv_store_pos = {}
        out_ops = []

        def wslab(w, nm, kc0, kc1, c0, ncols, dcol=0):
            src = w[nm + "_b"].rearrange("(c p) n -> p c n", p=128)[:, kc0:kc1, c0:c0 + ncols]
            n = kc1 - kc0
            return (lambda b: b[:, 0:n, dcol:dcol + ncols], src)

        def rmsnorm_to_xn(gcol, l):
            for c in range(8):
                ACT(sq.c(c % 2), hT.c(c), AF.Square)
                MM(PS(7), ones_b, sq.c(c % 2), start=(c == 0), stop=(c == 7))
            ACT(rs.c(0), PS(7), AF.Sqrt, bias=EPS, scale=1.0 / D)
            RCP(rstd.c(0), rs.c(0))
            for c in range(8):
                STT(xn.c(c), hT.c(c), V(vec[l].ap[:, 0, gcol + c:gcol + c + 1], vec[l].toks(0)), rstd.c(0), ALU.mult, ALU.mult)

        def ffn(l, which):
            w = W[l]
            n13, n2 = f"ffn{which}_w13", f"ffn{which}_w2"
            rmsnorm_to_xn(V_LN1 if which == 1 else V_LN2, l)
            for j in range(11):
                c0 = j * 256
                buf, tk = ws.get([wslab(w, n13, 0, 8, c0, 256, 0), wslab(w, n13, 0, 8, DFF + c0, 256, 256)], extraR=[w[n13 + "_tk"]])
                pb = 0 if j % 2 == 0 else 4
                for q in range(4):
                    for k in range(8):
                        MM(PS(pb + q), V(buf[:, k, q * 128:(q + 1) * 128], [tk]), xn.c(k), start=(k == 0), stop=(k == 7))
                for q in range(2):
                    b = (2 * j + q) % 2
                    ACT(sg.c(b), PS(pb + q), AF.Silu)
                    TT("dve", act.c(2 * j + q), sg.c(b), PS(pb + q + 2), ALU.mult)
            for half in range(2):
                pb = 0 if half == 0 else 4
                for (k0, k1) in [(0, 8), (8, 16), (16, 22)]:
                    buf, tk = ws.get([wslab(w, n2, k0, k1, half * 512, 512)], extraR=[w[n2 + "_tk"]])
                    for k in range(k0, k1):
                        for q in range(4):
                            MM(PS(pb + q), V(buf[:, k - k0, q * 128:(q + 1) * 128], [tk]), act.c(k), start=(k == 0), stop=(k == 21))
                for q in range(4):
                    c = half * 4 + q
                    STT(hT.c(c), PS(pb + q), 0.5, hT.c(c), ALU.mult, ALU.add)

        def vcol(l, col, p0=0, p1=128):
            return V(vec[l].ap[p0:p1, 0, col:col + 1], vec[l].toks(0))

        def rope_tables(t0):
            DMA("sp", posi.ap[0:64, 0, :], pos[0, t0:t0 + T].partition_broadcast(64), W=posi.toks(0), semtok=posi.toks(0)[0])
            ang = V(ra.ap[0:64, 0, :], ra.toks(0))
            kk = V(rb.ap[0:64, 0, :], rb.toks(0))
            CP("dve", ang, V(posi.ap[0:64, 0, :], posi.toks(0)))
            TS("dve", ang, ang, invf, ALU.mult)
            for (dst, shift) in ((sinS, 0.0), (cos2, math.pi / 2)):
                d = V(dst.ap[0:64, 0, :], dst.toks(0))
                TS("dve", kk, ang, 1.0 / TWO_PI, ALU.mult, shift / TWO_PI, ALU.add)
                TS("dve", kk, kk, 12582912.0, ALU.add)
                TS("dve", kk, kk, -12582912.0, ALU.add)
                STT(d, kk, -CW1, ang, ALU.mult, ALU.add)
                STT(d, kk, -CW2, d, ALU.mult, ALU.add)
                TS("dve", d, d, shift, ALU.add, -PI, ALU.max)
                TS("dve", d, d, PI, ALU.min)
                ACT(d, d, AF.Sin)
            TS("dve", V(sinS.ap[0:32, 0, :], sinS.toks(0)), V(sinS.ap[0:32, 0, :], sinS.toks(0)), -1.0, ALU.mult)

        def mla(l, it):
            w = W[l]
            t0 = it * T
            rope_tables(t0)
            cs = V(cos2.ap[0:64, 0, :], cos2.toks(0))
            sn = V(sinS.ap[0:64, 0, :], sinS.toks(0))
            wt = [w["w_in_tk"]]
            if KMLA < 1:
                return []
            buf, tk = ws.get([wslab(w, "w_in", 0, 8, 5120, 512)], extraR=wt)
            for q in range(4):
                for k in range(8):
                    MM(PS(q), V(buf[:, k, q * 128:(q + 1) * 128], [tk]), xn.c(k), start=(k == 0), stop=(k == 7))
            for q in range(4):
                if KV >= 2:
                    ACT(sq.c(q % 2), PS(q), AF.Square)
                    MM(PS(7), ones_b, sq.c(q % 2), start=(q == 0), stop=(q == 3))
                if KV >= 3:
                    ACT(cqn.c(q), PS(q), AF.Copy, scale=vcol(l, V_QLN + q))
            if KV >= 2:
                ACT(rs.c(0), PS(7), AF.Sqrt, bias=EPS, scale=1.0 / QL)
                RCP(rstd.c(0), rs.c(0))
            if KV >= 4:
                for q in range(4):
                    TT("dve", cqn.c(q), cqn.c(q), rstd.c(0), ALU.mult)
            if KMLA < 2:
                return []
            buf, tk = ws.get([wslab(w, "w_in", 0, 8, 5632, 384)], extraR=wt)
            for q in range(2):
                for k in range(8):
                    MM(PS(4 + q), V(buf[:, k, q * 128:(q + 1) * 128], [tk]), xn.c(k), start=(k == 0), stop=(k == 7))
            for k in range(8):
                MM(PS(6, 0, 512, 0, 64), V(buf[:, k, 256:320], [tk]), xn.c(k), start=(k == 0), stop=(k == 7))
            for k in range(8):
                MM(PS(3, 0, 512, 0, 64), V(buf[:, k, 320:384], [tk]), xn.c(k), start=(k == 0), stop=(k == 7))
            for q in range(2):
                ACT(sq.c(q % 2), PS(4 + q), AF.Square)
                MM(PS(7), ones_b, sq.c(q % 2), start=(q == 0), stop=(q == 1))
                ACT(ckvn.c(q), PS(4 + q), AF.Copy, scale=vcol(l, V_KVLN + q))
            ACT(rs.c(0), PS(7), AF.Sqrt, bias=EPS, scale=1.0 / KVL)
            RCP(rstd.c(0), rs.c(0))
            for q in range(2):
                TT("dve", ckvn.c(q), ckvn.c(q), rstd.c(0), ALU.mult)
            a64 = V(ra.ap[0:64, 0, :], ra.toks(0))
            b64 = V(rb.ap[0:64, 0, :], rb.toks(0))
            ACT(V(sq.ap[0:64, 0, :], sq.toks(0)), PS(6, 0, 512, 0, 64), AF.Square)
            ACT(a64, PS(6, 0, 512, 0, 64), AF.Copy, scale=vcol(l, V_KR, 0, 64))
            TT("dve", a64, a64, cs, ALU.mult)
            ACT(b64, PS(3, 0, 512, 0, 64), AF.Copy, scale=vcol(l, V_KRS, 0, 64))
            TT("dve", b64, b64, sn, ALU.mult)
            TT("dve", V(KrT.ap[0:64, 0, :], KrT.toks(0)), a64, b64, ALU.add)
            for cc in range(4):
                MM(PS(7, cc * 9 + 8, cc * 9 + 9), V(sq.ap[0:64, 0, cc * 128:(cc + 1) * 128], sq.toks(0)),
                   V(onesb.ap[0:64, 0, 0:1], onesb.toks(0)))
            if KMLA < 3:
                return []
            wt2 = [w["w_ukv_tk"]]
            for half in range(2):
                buf, tk = ws.get([wslab(w, "w_ukv", 0, 2, half * 512, 512)], extraR=wt2)
                for q in range(4):
                    h = half * 4 + q
                    pb = q % 2
                    for k in range(2):
                        MM(PS(pb), V(buf[:, k, q * 128:(q + 1) * 128], [tk]), ckvn.c(k), start=(k == 0), stop=(k == 1))
                    ACT(sq.c(pb), PS(pb), AF.Square)
                    for cc in range(4):
                        MM(PS(7, cc * 9 + h, cc * 9 + h + 1), sq.c(pb, cc * 128, (cc + 1) * 128), V(onesb.ap[:, 0, 0:1], onesb.toks(0)))
                    ACT(KTs.c(h), PS(pb), AF.Copy, scale=vcol(l, V_KN))
            CP("act", V(ksq.ap[:, 0, 64:100], ksq.toks(0)), PS(7, 0, 36))
            for cc in range(4):
                kq = V(ksq.ap[:, 0, cc * 8:(cc + 1) * 8], ksq.toks(0))
                TS("dve", kq, V(ksq.ap[:, 0, 64 + cc * 9:64 + cc * 9 + 8], ksq.toks(0)),
                   V(ksq.ap[:, 0, 64 + cc * 9 + 8:64 + cc * 9 + 9], ksq.toks(0)), ALU.add)
                ACT(kq, kq, AF.Sqrt, bias=EPS * 192.0, scale=1.0)
                RCP(sck.c(it * 4 + cc), kq)
            if KMLA < 4:
                return []
            for half in range(2):
                buf, tk = ws.get([wslab(w, "w_ukv", 0, 2, 1024 + half * 512, 512)], extraR=wt2)
                for cc in range(4):
                    pb = 2 + cc
                    for k in range(2):
                        MM(PS(pb), ckvn.c(k, cc * 128, (cc + 1) * 128), V(buf[:, k, :], [tk]), start=(k == 0), stop=(k == 1))
                    CP("act", Vs.c(cc, half * 512, (half + 1) * 512), PS(pb))
            if KMLA < 5:
                return []
            st_ops = []
            for h in range(MH):
                st_ops.append(DMA("pool", KN[h, it], KTs.ap[:, h, :], R=KTs.toks(h), W=[tk_kvc[it]], semtok=tk_kvc[it], grp=("kvst", l, it)))
                st_ops.append(DMA("pool", VV[h, it], Vs.ap[:, :, h * 128:(h + 1) * 128], R=Vs.toks(0, 4), W=[tk_kvc[it]], semtok=tk_kvc[it], grp=("kvst", l, it)))
            st_ops.append(DMA("pool", KR[it], KrT.ap[0:64, 0, :], R=KrT.toks(0), W=[tk_kvc[it]], semtok=tk_kvc[it], grp=("kvst", l, it)))
            kv_store_pos[it] = len(P.ops)
            if KMLA < 6:
                return []
            wq = [w["w_uq_tk"]]
            slabs = {}
            for h in range(MH):
                if h % 4 == 0:
                    bn, tn = ws.get([wslab(w, "w_uq", 0, 4, (h // 4) * 512, 512)], extraR=wq)
                if h % 8 == 0:
                    br, tr = ws.get([wslab(w, "w_uq", 0, 4, 1024, 512)], extraR=wq)
                    bs, ts_ = ws.get([wslab(w, "w_uq", 0, 4, 1536, 512)], extraR=wq)
                hq = h % 4
                for k in range(4):
                    MM(PS(0), V(bn[:, k, hq * 128:(hq + 1) * 128], [tn]), cqn.c(k), start=(k == 0), stop=(k == 3))
                for k in range(4):
                    MM(PS(1, 0, 512, 0, 64), V(br[:, k, h * 64:(h + 1) * 64], [tr]), cqn.c(k), start=(k == 0), stop=(k == 3))
                for k in range(4):
                    MM(PS(2, 0, 512, 0, 64), V(bs[:, k, h * 64:(h + 1) * 64], [ts_]), cqn.c(k), start=(k == 0), stop=(k == 3))
                ACT(sq.c(0), PS(0), AF.Square)
                ACT(V(sq.ap[0:64, 1, :], sq.toks(1)), PS(1, 0, 512, 0, 64), AF.Square)
                MM(PS(3), ones_b, sq.c(0), start=True, stop=False)
                MM(PS(3), V(onesb.ap[0:64, 0, :], onesb.toks(0)), V(sq.ap[0:64, 1, :], sq.toks(1)), start=False, stop=True)
                ACT(rs.c(0), PS(3), AF.Sqrt, bias=EPS, scale=1.0 / 192.0)
                RCP(rstdq.c(0), rs.c(0))
                ACT(sg.c(0), PS(0), AF.Copy, scale=vcol(l, V_QN))
                TT("dve", Qn.c(h), sg.c(0), rstdq.c(0), ALU.mult)
                ACT(a64, PS(1, 0, 512, 0, 64), AF.Copy, scale=vcol(l, V_QR, 0, 64))
                TT("dve", a64, a64, cs, ALU.mult)
                ACT(b64, PS(2, 0, 512, 0, 64), AF.Copy, scale=vcol(l, V_QRS, 0, 64))
                TT("dve", b64, b64, sn, ALU.mult)
                TT("pool", a64, a64, b64, ALU.add)
                TT("pool", V(Qr.ap[0:64, h, :], Qr.toks(h)), a64, V(rstdq.ap[0:64, 0, :], rstdq.toks(0)), ALU.mult)
            if KMLA < 7:
                return []
            nokv = bool(_os.environ.get("KNOKV"))
            for h in range(MH):
                po, pd = (0, 1) if h % 2 == 0 else (5, 6)
                blocks = [(j, b) for j in range(it + 1) for b in range(4)]
                nb_ = len(blocks)
                ops_ = {}

                def S_(n):
                    j, b = blocks[n]
                    if j < it and not nokv and b == 0:
                        kb, kt = kvs.get([
                            (lambda b_: b_[:, 0:512], KN[h, j]),
                            (lambda b_: b_[0:64, 512:1024], KR[j]),
                            (lambda b_: b_[:, 1024:1536].rearrange("p (c d) -> p c d", d=128), VV[h, j]),
                        ], extraR=[tk_kvc[j]], minpos=kv_store_pos[j])
                        ops_[j] = (kb, kt)
                    q0 = 128 * b if j == it else 0
                    si = 2 + (n % 3)
                    pt = n % 3
                    if j < it and not nokv:
                        kb, kt = ops_[j]
                        kn_ = V(kb[:, b * 128:(b + 1) * 128], [kt])
                        kr_ = V(kb[0:64, 512 + b * 128:512 + (b + 1) * 128], [kt])
                    else:
                        kn_ = KTs.c(h, b * 128, (b + 1) * 128)
                        kr_ = V(KrT.ap[0:64, 0, b * 128:(b + 1) * 128], KrT.toks(0))
                    MM(PS(si, q0, 512), kn_, Qn.c(h, q0, 512), start=True, stop=False)
                    MM(PS(si, q0, 512), kr_, V(Qr.ap[0:64, h, q0:512], Qr.toks(h)), start=False, stop=True)
                    ACT(PT.c(pt, q0, 512), PS(si, q0, 512), AF.Exp, scale=V(sck.ap[:, j * 4 + b, h:h + 1], sck.toks(j * 4 + b)))
                    if j == it:
                        TT("pool", PT.c(pt, q0, q0 + 128), PT.c(pt, q0, q0 + 128), U_b, ALU.mult)

                def PV_(n):
                    j, b = blocks[n]
                    q0 = 128 * b if j == it else 0
                    pt = n % 3
                    if j < it and not nokv:
                        kb, kt = ops_[j]
                        v_ = V(kb[:, 1024 + b * 128:1024 + (b + 1) * 128], [kt])
                    else:
                        v_ = Vs.c(b, h * 128, (h + 1) * 128)
                    MM(PS(po, q0, 512), v_, PT.c(pt, q0, 512), start=(n == 0), stop=(n == nb_ - 1))
                    MM(PS(pd, q0, 512), ones_b, PT.c(pt, q0, 512), start=(n == 0), stop=(n == nb_ - 1))

                for n in range(nb_ + 2):
                    if n < nb_:
                        S_(n)
                    if n >= 2:
                        PV_(n - 2)
                RCP(rd.c(h % 2), PS(pd))
                TT("dve", oTb.c(h), PS(po), rd.c(h % 2), ALU.mult)
            for half in range(2):
                buf, tk = ws.get([wslab(w, "w_mla_out", 0, 8, half * 512, 512)], extraR=[w["w_mla_out_tk"]])
                for q in range(4):
                    for k in range(8):
                        MM(PS(q), V(buf[:, k, q * 128:(q + 1) * 128], [tk]), oTb.c(k), start=(k == 0), stop=(k == 7))
                buf, tk = ws.get([wslab(w, "w_in", 0, 8, 6048 + 1024 + half * 512, 512)], extraR=wt)
                for q in range(4):
                    for k in range(8):
                        MM(PS(4 + q), V(buf[:, k, q * 128:(q + 1) * 128], [tk]), xn.c(k), start=(k == 0), stop=(k == 7))
                for q in range(4):
                    ACT(sg.c(q % 2), PS(4 + q), AF.Sigmoid)
                    TT("dve", mg.c(half * 4 + q), sg.c(q % 2), PS(q), ALU.mult)
            return st_ops

        ssd_start = [0]

        def ssd(l, it):
            w = W[l]
            ssd_start[0] = len(P.ops)
            wt = [w["w_in_tk"]]
            if KV == 8:
                ACT(V(rs.ap[:, 0, 0:32], rs.toks(0)), V(rs.ap[:, 0, 32:64], rs.toks(0)), AF.Copy)
            if KV == 9:
                ACT(V(rs.ap[:, 0, 0:32], rs.toks(0)), V(rs.ap[:, 0, 32:64], rs.toks(0)), AF.Copy)
                ACT(V(rs.ap[:, 0, 0:32], rs.toks(0)), V(rs.ap[:, 0, 32:64], rs.toks(0)), AF.Copy)
            for nb in [int(c_) for c_ in _os.environ.get("KNB", "0123")]:
                buf, tk = ws.get([wslab(w, "w_in", 0, 8, nb * 512, 512)], extraR=wt)
                for cc in range(4):
                    pb = (nb % 2) * 4 + cc
                    for k in range(8):
                        MM(PS(pb), xn.c(k, cc * 128, (cc + 1) * 128), V(buf[:, k, :], [tk]), start=(k == 0), stop=(k == 7))
                    ACT(zs.c(cc, nb * 512, (nb + 1) * 512), PS(pb), AF.Silu)
            def cstage1(ch, buf, tk):
                q = ch % 4
                sb = ch // 4
                pb = (sb % 2) * 4 + q
                for k in range(8):
                    MM(PS(pb), V(buf[:, k, q * 128:(q + 1) * 128], [tk]), xn.c(k), start=(k == 0), stop=(k == 7))
                s_ = ch % 2
                CP("act", stg.c(s_, 3, 515), PS(pb))
                CP("pool", stg.c(s_, 0, 3), halo.c(ch))

            def cstage2(ch):
                s_ = ch % 2
                ca = cacc.c(s_)
                TS("dve", ca, stg.c(s_, 3, 515), vcol(l, V_CW + ch * 4 + 3), ALU.mult)
                STT(ca, stg.c(s_, 2, 514), vcol(l, V_CW + ch * 4 + 2), ca, ALU.mult, ALU.add)
                STT(ca, stg.c(s_, 1, 513), vcol(l, V_CW + ch * 4 + 1), ca, ALU.mult, ALU.add)
                STT(ca, stg.c(s_, 0, 512), vcol(l, V_CW + ch * 4 + 0), ca, ALU.mult, ALU.add)
                CP("pool", halo.c(ch), stg.c(s_, 512, 515))
                dst = xsT.c(ch) if ch < 16 else (BT.c(ch - 16) if ch < 20 else CT.c(ch - 20))
                ACT(dst, ca, AF.Silu, bias=vcol(l, V_CB + ch))

            cur = None
            for ch in range(24):
                if ch % 4 == 0:
                    cur = ws.get([wslab(w, "w_in", 0, 8, 2048 + (ch // 4) * 512, 512)], extraR=wt)
                cstage1(ch, cur[0], cur[1])
                if ch >= 1:
                    cstage2(ch - 1)
            cstage2(23)
            bufd, tkd = ws.get([wslab(w, "w_in", 0, 8, 6016, 32)], extraR=wt)
            rw = rowsb[l]
            dtb3 = V(rw.ap[:, 0, 0:32].unsqueeze(1).broadcast_to([128, 4, 32]), rw.toks(0))
            an3 = V(aneg[l].ap[:, 0, :].unsqueeze(1).broadcast_to([128, 4, 32]), aneg[l].toks(0))
            Dv = V(rw.ap[:, 0, 64:96], rw.toks(0))
            for cc in range(4):
                for k in range(8):
                    MM(PS(7, cc * 32, cc * 32 + 32), xn.c(k, cc * 128, (cc + 1) * 128), V(bufd[:, k, 0:32], [tkd]), start=(k == 0), stop=(k == 7))
            dtv, dA, nac, eac, wv, ela, dtw, tmp8 = [dts.c(i) for i in range(8)]
            v3 = lambda v_: V(v_.ap.rearrange("p (c h) -> p c h", h=32), v_.tk)
            TT("dve", v3(dtv), v3(PS(7, 0, 128)), dtb3, ALU.add)
            ACT(dtv, dtv, AF.Exp)
            ACT(dtv, dtv, AF.Ln, bias=1.0)
            TT("dve", v3(dA), v3(dtv), an3, ALU.mult)
            for cc in range(4):
                MM(PS(7, 128 + cc * 32, 160 + cc * 32), U_f, V(dA.ap[:, cc * 32:(cc + 1) * 32], dA.tk))
                MM(PS(7, 256 + cc * 32, 288 + cc * 32), ones_f, V(dA.ap[:, cc * 32:(cc + 1) * 32], dA.tk))
            TS("dve", nac, PS(7, 128, 256), -1.0, ALU.mult)
            ACT(eac, PS(7, 128, 256), AF.Exp)
            TT("dve", tmp8, nac, PS(7, 256, 384), ALU.add)
            ACT(wv, tmp8, AF.Exp)
            ACT(ela, PS(7, 256, 384), AF.Exp)
            TT("dve", dtw, dtv, wv, ALU.mult)
            r3 = lambda v_: V(v_.ap.rearrange("p (h d) -> p h d", d=64), v_.tk)
            ssd0 = ssd_start[0]
            if _os.environ.get("KMARK"):
                print("MARK dt-hoist end", len(P.ops) - ssd0)
            for cc in range(4):
                if _os.environ.get("KMARK"):
                    print("MARK chunk", cc, len(P.ops) - ssd0)
                csl = slice(cc * 32, (cc + 1) * 32)
                dtv_c = V(dtv.ap[:, csl], dtv.tk)
                dtw_c = V(dtw.ap[:, csl], dtw.tk)
                dA_c = V(dA.ap[:, csl], dA.tk)
                nac_c = V(nac.ap[:, csl], nac.tk)
                eac_c = V(eac.ap[:, csl], eac.tk)
                ela_c = V(ela.ap[:, csl], ela.tk)
                for f in range(16):
                    TR(PSB(f // 8, (f % 8) * 128, (f % 8 + 1) * 128), xsT.c(f, cc * 128, (cc + 1) * 128), ident_b)
                for g in range(NG):
                    TR(PSB(2, g * 128, (g + 1) * 128), BT.c(g, cc * 128, (cc + 1) * 128), ident_b)
                for hb in range(2):
                    src = V(psb[hb][:, :].rearrange("p (h d) -> p h d", d=64), [tk_ps[hb]])
                    hs = slice(hb * 16, (hb + 1) * 16)
                    cs_ = slice(hb * 1024, (hb + 1) * 1024)
                    bc = lambda b_: V(b_.ap[:, hs].unsqueeze(2).broadcast_to([128, 16, 64]), b_.tk)
                    o3 = lambda b_: V(b_.ap[:, 0, cs_].rearrange("p (h d) -> p h d", d=64), b_.toks(0))
                    TT("dve", o3(xdt), src, bc(dtv_c), ALU.mult)
                    TT("dve", o3(xw), src, bc(dtw_c), ALU.mult)
                    TT("dve", o3(xtok), src, bc(V(Dv.ap, Dv.tk)), ALU.mult)
                CP("dve", Btok.c(0), PSB(2, 0, 512))
                for g in range(NG):
                    MM(PS(3, g * 128, (g + 1) * 128), BT.c(g, cc * 128, (cc + 1) * 128), CT.c(g, cc * 128, (cc + 1) * 128))
                CP("act", CBm.c(0), PS(3))

                def front(g):
                    e_ = g % 2
                    ab = (4, 5) if e_ == 0 else (0, 1)
                    U3 = V(cstb.ap[:, 0, C_U:C_U + 128].unsqueeze(1).broadcast_to([128, 8, 128]), cstb.toks(0))
                    dA3 = V(dA_c.ap[:, g * 8:(g + 1) * 8].unsqueeze(2).broadcast_to([128, 8, 128]), dA_c.tk)
                    TT("pool", V(dAU.ap[:, e_, :].rearrange("p (h t) -> p h t", t=128), dAU.toks(e_)), U3, dA3, ALU.mult)
                    for q in range(2):
                        MM(PS(ab[q]), ones_f, dAU.c(e_, q * 512, (q + 1) * 512), start=True, stop=False)
                        MM(PS(ab[q]), ident_b, negm.c(0), start=False, stop=True)
                    for hh in range(8):
                        h = g * 8 + hh
                        ACT(Eb.c(e_, hh * 128, (hh + 1) * 128), PS(ab[hh // 4], (hh % 4) * 128, (hh % 4 + 1) * 128), AF.Exp,
                            bias=V(nac_c.ap[:, h:h + 1], nac_c.tk))
                    E3 = V(Eb.ap[:, e_, :].rearrange("p (h t) -> p h t", t=128), Eb.toks(e_))
                    M3 = V(Mt.ap[:, e_, :].rearrange("p (h t) -> p h t", t=128), Mt.toks(e_))
                    CB3 = V(CBm.ap[:, 0, g * 128:(g + 1) * 128].unsqueeze(1).broadcast_to([128, 8, 128]), CBm.toks(0))
                    TT("dve", M3, E3, CB3, ALU.mult)

                def back(g):
                    e_ = g % 2
                    for hh in range(8):
                        h = g * 8 + hh
                        if hh == 0:
                            MM(PS(6), ident_b, xtok.c(0, g * 512, (g + 1) * 512), start=True, stop=False)
                        MM(PS(6, hh * 64, (hh + 1) * 64), Mt.c(e_, hh * 128, (hh + 1) * 128), xdt.c(0, h * 64, (h + 1) * 64),
                           start=False, stop=(hh == 7))
                    MM(PS(2), CT.c(g, cc * 128, (cc + 1) * 128), Sb.c(g))
                    eb3 = V(eac_c.ap[:, g * 8:(g + 1) * 8].unsqueeze(2).broadcast_to([128, 8, 64]), eac_c.tk)
                    TT("dve", r3(tmpa.c(e_)), r3(PS(2)), eb3, ALU.mult)
                    TT("dve", yb.c(0, g * 512, (g + 1) * 512), PS(6), tmpa.c(e_), ALU.add)
                    MM(PS(3), Btok.c(0, g * 128, (g + 1) * 128), xw.c(0, g * 512, (g + 1) * 512))
                    el3 = V(ela_c.ap[:, g * 8:(g + 1) * 8].unsqueeze(2).broadcast_to([128, 8, 64]), ela_c.tk)
                    TT("pool", r3(Sst.c(g)), r3(Sst.c(g)), el3, ALU.mult)
                    TT("dve", Sst.c(g), Sst.c(g), PS(3), ALU.add)
                    CP("act", Sb.c(g), Sst.c(g))

                front(0)
                front(1)
                back(0)
                front(2)
                back(1)
                front(3)
                back(2)
                back(3)
                TT("dve", yb.c(0), yb.c(0), zs.c(cc), ALU.mult)
                ssqy = V(ksq2.ap[:, 0, 0:1], ksq2.toks(0))
                ACT(ynb.c(0), yb.c(0), AF.Square, accum=ssqy)
                ACT(ssqy, ssqy, AF.Ln, bias=epsc, scale=1.0 / DIN)
                ACT(V(ksq2.ap[:, 0, 1:2], ksq2.toks(0)), ssqy, AF.Exp, scale=-0.5)
                ACT(ynb.c(0), yb.c(0), AF.Copy, scale=V(ksq2.ap[:, 0, 1:2], ksq2.toks(0)))
                for f in range(16):
                    TR(PSB(f // 8, (f % 8) * 128, (f % 8 + 1) * 128), ynb.c(0, f * 128, (f + 1) * 128), ident_b)
                for hb in range(2):
                    CP("dve", V(ynT.ap[:, hb * 8:(hb + 1) * 8, cc * 128:(cc + 1) * 128], ynT.toks(hb * 8, hb * 8 + 8)),
                       V(psb[hb][:, :].rearrange("p (f t) -> p f t", t=128), [tk_ps[hb]]))
            for f in range(16):
                TS("pool", ynT.c(f), ynT.c(f), vcol(l, V_SSDN + f), ALU.mult)
            if _os.environ.get("KMARK"):
                print("MARK outproj", len(P.ops) - ssd0)
            for half in range(2):
                for kg in range(2):
                    buf, tk = ws.get([wslab(w, "w_ssd_out", kg * 8, kg * 8 + 8, half * 512, 512)], extraR=[w["w_ssd_out_tk"]])
                    for k in range(8):
                        for q in range(4):
                            MM(PS(q), V(buf[:, k, q * 128:(q + 1) * 128], [tk]), ynT.c(kg * 8 + k), start=(kg == 0 and k == 0), stop=(kg == 1 and k == 7))
                buf, tk = ws.get([wslab(w, "w_in", 0, 8, 6048 + half * 512, 512)], extraR=wt)
                for q in range(4):
                    for k in range(8):
                        MM(PS(4 + q), V(buf[:, k, q * 128:(q + 1) * 128], [tk]), xn.c(k), start=(k == 0), stop=(k == 7))
                for q in range(4):
                    c = half * 4 + q
                    ACT(sg.c(q % 2), PS(4 + q), AF.Sigmoid)
                    TT("dve", sg.c(q % 2), sg.c(q % 2), PS(q), ALU.mult)
                    TT("pool", mg.c(c), mg.c(c), sg.c(q % 2), ALU.add)
            for half in range(2):
                buf, tk = ws.get([wslab(w, "w_o", 0, 8, half * 512, 512)], extraR=[w["w_o_tk"]])
                pb = half * 4
                for q in range(4):
                    for k in range(8):
                        MM(PS(pb + q), V(buf[:, k, q * 128:(q + 1) * 128], [tk]), mg.c(k), start=(k == 0), stop=(k == 7))
                for q in range(4):
                    c = half * 4 + q
                    TT("dve", hT.c(c), hT.c(c), PS(pb + q), ALU.add)

        dbg_col = [0]

        def dump(v, n):
            if dbg_out is None:
                return
            c0 = dbg_col[0]
            dbg_col[0] += n
            out_ops.append(DMA("pool", dbg_out[0:v.ap.shape[0], c0:c0 + n], v.ap, R=v.tk, semtok=Tk("dbgs")))

        for l in range(DEPTH):
            for g in range(NG):
                MEMSET("pool", Sst.c(g), 0.0)
                MEMSET("pool", Sb.c(g), 0.0)
            MEMSET("pool", halo.all(), 0.0)
            for it in range(NT):
                t0 = it * T
                if l == 0:
                    DMA("sp", hT.ap, xTv[:, :, t0:t0 + T], W=hT.toks(0, 8), semtok=hT.toks(0)[0])
                else:
                    DMA("sp", hT.ap, hbv[:, :, t0:t0 + T], R=[tk_hb[it]], W=hT.toks(0, 8), semtok=hT.toks(0)[0])
                if "f1" in PHASES:
                    ffn(l, 1)
                if "nm" in PHASES:
                    rmsnorm_to_xn(V_LNM, l)
                if "mla" in PHASES:
                    mla(l, it)
                if "ssd" in PHASES:
                    if KSSD < 999999:
                        P.limit = len(P.ops) + KSSD
                    ssd(l, it)
                    P.limit = None
                if "f2" in PHASES:
                    ffn(l, 2)
                if l == DEPTH - 1:
                    out_ops.append(DMA("pool", oTv[:, :, t0:t0 + T], hT.ap, R=hT.toks(0, 8), semtok=hT.toks(1)[0]))
                else:
                    DMA("pool", hbv[:, :, t0:t0 + T], hT.ap, R=hT.toks(0, 8), W=[tk_hb[it]], semtok=hT.toks(1)[0])
        fence = P.add("pool", None)
        ws.flush()
        kvs.flush()
        P.finalize()
        lastdma = {}
        for op_ in P.final:
            if op_.dma:
                k_ = id(op_.sem)
                if k_ not in lastdma or lastdma[k_].cnt <= op_.cnt:
                    lastdma[k_] = op_
        fence.deps = list(out_ops) + list(lastdma.values())
        if _os.environ.get("KDUMP"):
            n0 = int(_os.environ["KDUMP"])
            idx = {id(op): i for i, op in enumerate(P.final)}
            for i, op in enumerate(P.final[-n0:]):
                ds = [(idx.get(id(d), -1), d.eng, d.name, d.cnt) for d in op.deps]
                print(len(P.final) - n0 + i, op.eng, op.name, "dma" if op.dma else "", "sig" if op.sig else "", op.cnt,
                      [t.name for t in op.W][:3], "<-", ds)
        P.emit()
        build.stats = (len(P.final), P.nsem)
    return nc


def _consts():
    c = np.zeros((128, NCST), np.float32)
    c[:, C_ID:C_ID + 128] = np.eye(128, dtype=np.float32)
    c[:, C_U:C_U + 128] = np.triu(np.ones((128, 128), np.float32))
    inv = (1.0 / (np.float32(10000.0) ** (np.arange(0, 64, 2, dtype=np.float32) / np.float32(64)))).astype(np.float32)
    c[0:32, C_INV] = inv
    c[32:64, C_INV] = inv
    return c


def _pack_layer(inputs, l):
    g = lambda n: np.asarray(inputs[n][l], np.float32)
    vec = np.zeros((128, NV), np.float32)
    col = lambda v: np.ascontiguousarray(v.reshape(-1, 128).T)
    vec[:, V_LN1:V_LN1 + 8] = col(g("ln_ffn1"))
    vec[:, V_LNM:V_LNM + 8] = col(g("ln_mix"))
    vec[:, V_LN2:V_LN2 + 8] = col(g("ln_ffn2"))
    cw = g("conv_w")
    vec[:, V_CW:V_CW + 96] = cw.reshape(4, 24, 128).transpose(2, 1, 0).reshape(128, 96)
    vec[:, V_CB:V_CB + 24] = col(g("conv_b"))
    vec[:, V_QLN:V_QLN + 4] = col(g("q_lora_norm"))
    vec[:, V_KVLN:V_KVLN + 2] = col(g("kv_lora_norm"))
    qn, kn = g("q_norm"), g("k_norm")
    for (base, v) in ((V_QN, qn), (V_KN, kn)):
        vec[:, base] = v[0:128]
        vec[0:64, base + 1] = v[128:192]
        vec[0:32, base + 2] = v[160:192]
        vec[32:64, base + 2] = v[128:160]
    vec[:, V_SSDN:V_SSDN + 16] = col(g("ssd_norm"))
    rows = np.concatenate([g("dt_bias"), g("a_log"), g("d_skip")]).reshape(1, 96).astype(np.float32)
    d = {f"vecs{l}": vec, f"rows{l}": rows}
    for nm in ["ffn1_w13", "ffn1_w2", "w_in", "w_ssd_out", "w_uq", "w_ukv", "w_mla_out", "w_o", "ffn2_w13", "ffn2_w2"]:
        d[f"{nm}{l}"] = np.ascontiguousarray(g(nm))
    return d


_NC_CACHE = {}


def kernel(**inputs):
    x = np.asarray(inputs["x"], np.float32)
    B, SEQ, _ = x.shape
    DEPTH = inputs["ln_ffn1"].shape[0]
    key = (SEQ, DEPTH)
    if key not in _NC_CACHE:
        _NC_CACHE[key] = build(SEQ, DEPTH)
    nc = _NC_CACHE[key]
    shared = {"cst": _consts()}
    for l in range(DEPTH):
        shared.update(_pack_layer(inputs, l))
    in_maps = []
    for b in range(B):
        m = dict(shared)
        m["xT"] = np.ascontiguousarray(x[b].T)
        m["pos"] = np.ascontiguousarray(np.asarray(inputs["positions"][b], np.int32).reshape(1, SEQ))
        in_maps.append(m)
    res = run_bass_kernel_spmd(nc, in_maps, core_ids=list(range(B)))
    out = np.stack([np.ascontiguousarray(res.results[b]["oT"].T) for b in range(B)], axis=0)
    return out.astype(np.float32)
```

```python
import math
import time
from contextlib import ExitStack

import numpy as np
import concourse.bass as bass
import concourse.mybir as mybir
from concourse.bass_utils import run_bass_kernel_spmd

dt = mybir.dt
F32, BF16, I32 = dt.float32, dt.bfloat16, dt.int32
AF = mybir.ActivationFunctionType
ALU = mybir.AluOpType

SAME_ENG_SYNC = True
import os as _os
PHASES = _os.environ.get("KPH", "f1,nm,mla,ssd,f2").split(",")
KMLA = int(_os.environ.get("KMLA", "99"))
KSSD = int(_os.environ.get("KSSD", "999999"))
KV = int(_os.environ.get("KV", "99"))


class Tk:
    __slots__ = ("name", "lw", "rd", "sem", "dcnt")

    def __init__(self, name):
        self.name = name
        self.lw = None
        self.rd = []
        self.sem = None
        self.dcnt = 0


class Op:
    __slots__ = ("eng", "fn", "R", "W", "dma", "deps", "sig", "sem", "cnt", "name", "grp", "semtok")


class Prog:
    ENGS = ["pe", "act", "dve", "pool", "sp"]

    def __init__(self, nc, stack):
        self.nc = nc
        self.stack = stack
        self.ops = []
        self.inserts = {}
        self.nsem = 0

    def sbuf(self, name, shape, dtype):
        return self.stack.enter_context(self.nc.sbuf_tensor(name, list(shape), dtype))

    def psum(self, name, shape, dtype):
        return self.stack.enter_context(self.nc.psum_tensor(name, list(shape), dtype))

    def sem(self, name):
        self.nsem += 1
        return self.stack.enter_context(self.nc.semaphore(name))

    def mkop(self, eng, fn, R=(), W=(), dma=False, name="", semtok=None, grp=None):
        op = Op()
        op.eng, op.fn, op.R, op.W, op.dma, op.name = eng, fn, tuple(R), tuple(W), dma, name
        op.sig = False
        op.sem = None
        op.cnt = 0
        op.grp = grp
        op.semtok = semtok
        return op

    limit = None

    def add(self, *a, **k):
        op = self.mkop(*a, **k)
        if self.limit is not None and len(self.ops) >= self.limit:
            return op
        self.ops.append(op)
        return op

    def insert_before(self, pos, op):
        self.inserts.setdefault(pos, []).append(op)

    def finalize(self):
        final = []
        for i, op in enumerate(self.ops):
            if i in self.inserts:
                final.extend(self.inserts[i])
            final.append(op)
        if len(self.ops) in self.inserts:
            final.extend(self.inserts[len(self.ops)])
        self.final = final
        for op in final:
            deps = {}
            for t in op.R:
                if t.lw is not None:
                    deps[id(t.lw)] = t.lw
            for t in op.W:
                if t.lw is not None:
                    deps[id(t.lw)] = t.lw
                for x in t.rd:
                    deps[id(x)] = x
            deps.pop(id(op), None)
            op.deps = [d for d in deps.values() if not (op.grp is not None and d.grp == op.grp)]
            for t in op.R:
                if op.dma:
                    t.rd.append(op)
                else:
                    t.rd = [x for x in t.rd if x.dma or x.eng != op.eng]
                    t.rd.append(op)
            for t in op.W:
                t.lw = op
                t.rd = []
        for op in final:
            if op.dma:
                tok = op.semtok if op.semtok is not None else (op.W[0] if op.W else op.R[0])
                if tok.sem is None:
                    tok.sem = self.sem("d_" + tok.name)
                tok.dcnt += 16
                op.sem = tok.sem
                op.cnt = tok.dcnt
        gmax = {}
        for op in final:
            if op.dma and op.grp is not None:
                k = (op.grp, id(op.sem))
                gmax[k] = max(gmax.get(k, 0), op.cnt)
        for op in final:
            if op.dma and op.grp is not None:
                op.cnt = gmax[(op.grp, id(op.sem))]
        for op in final:
            for d in op.deps:
                if d.dma:
                    continue
                if d.eng != op.eng:
                    d.sig = True
                elif SAME_ENG_SYNC and op.eng != "pe" and not op.dma:
                    d.sig = True
        self.esem = {e: self.sem("e_" + e) for e in self.ENGS}
        ecnt = {e: 0 for e in self.ENGS}
        for op in final:
            if not op.dma and op.sig:
                ecnt[op.eng] += 1
                op.sem = self.esem[op.eng]
                op.cnt = ecnt[op.eng]
        self.by_eng = {e: [op for op in final if op.eng == e] for e in self.ENGS}

    def _run(self, ename, e):
        known = {}
        for op in self.by_eng[ename]:
            waits = {}
            for d in op.deps:
                if not d.dma:
                    if d.eng == ename and (ename == "pe" or not SAME_ENG_SYNC or op.dma):
                        continue
                    if not d.sig:
                        continue
                k = id(d.sem)
                if k not in waits or waits[k][1] < d.cnt:
                    waits[k] = (d.sem, d.cnt)
            for k, (s, v) in waits.items():
                if known.get(k, 0) >= v:
                    continue
                known[k] = v
                e.wait_ge(s, v)
            if op.fn is None:
                continue
            ins = op.fn(e)
            if op.dma:
                ins.then_inc(op.sem, 16)
            elif op.sig:
                ins.then_inc(op.sem, 1)

    def emit(self):
        with self.nc.Block() as blk:
            @blk.tensor
            def _(e):
                self._run("pe", e)

            @blk.scalar
            def _(e):
                self._run("act", e)

            @blk.vector
            def _(e):
                self._run("dve", e)

            @blk.gpsimd
            def _(e):
                self._run("pool", e)

            @blk.sync
            def _(e):
                self._run("sp", e)


class WStream:
    def __init__(self, P, name, shape, dtype, nslots, la, eng="sp"):
        self.P = P
        self.name = name
        self.n = nslots
        self.la = la
        self.eng = eng
        self.buf = [P.sbuf(f"{name}{i}", shape, dtype) for i in range(nslots)]
        self.tok = [Tk(f"{name}{i}") for i in range(nslots)]
        self.reqs = []
        self.minpos = []

    def get(self, pieces, extraR=(), minpos=0):
        if self.P.limit is not None and len(self.P.ops) >= self.P.limit:
            return self.buf[0], self.tok[0]
        n = len(self.reqs)
        self.minpos.append(minpos)
        self.reqs.append((len(self.P.ops), pieces, tuple(extraR)))
        s = n % self.n
        return self.buf[s], self.tok[s]

    def flush(self):
        P = self.P
        import bisect
        tid = {id(t): i for i, t in enumerate(self.tok)}
        reads = [[] for _ in self.tok]
        for pos, op in enumerate(P.ops):
            for t in op.R:
                i = tid.get(id(t))
                if i is not None:
                    reads[i].append(pos)
        nreq = len(self.reqs)
        last_use = [0] * nreq
        for n in range(nreq):
            s = n % self.n
            lo = self.reqs[n][0]
            hi = self.reqs[n + self.n][0] if n + self.n < nreq else len(P.ops) + 1
            r = reads[s]
            a = bisect.bisect_left(r, lo)
            b = bisect.bisect_left(r, hi)
            last_use[n] = r[b - 1] if b > a else lo
        for n, (pos, pieces, extraR) in enumerate(self.reqs):
            s = n % self.n
            ipos = self.reqs[max(0, n - self.la)][0]
            if n - self.n >= 0:
                ipos = max(ipos, last_use[n - self.n] + 1)
            ipos = max(ipos, self.minpos[n])
            assert ipos <= pos, (self.name, n, ipos, pos)
            for (dstf, src) in pieces:
                dst = dstf(self.buf[s])
                op = P.mkop(self.eng, (lambda e, d=dst, s_=src: e.dma_start(out=d, in_=s_)),
                            R=extraR, W=[self.tok[s]], dma=True, name=f"ld_{self.name}{n}",
                            semtok=self.tok[s], grp=(self.name, n))
                P.insert_before(ipos, op)


class V:
    __slots__ = ("ap", "tk")

    def __init__(self, ap, tk):
        self.ap = ap
        self.tk = list(tk)


class Arena:
    GR = 1024

    def __init__(self, P, name, nbytes):
        self.nbytes = nbytes
        self.t = P.sbuf(name, [128, nbytes // 4], F32)
        self.tb = self.t.bitcast(BF16)
        self.ti = self.t.bitcast(I32)
        self.g = [Tk(f"{name}_g{i}") for i in range((nbytes + self.GR - 1) // self.GR)]

    def toks(self, lo, hi):
        return self.g[lo // self.GR:(hi + self.GR - 1) // self.GR]


class Buf:
    def __init__(self, P, name, C, T, dtype, arena=None, off=None):
        esz = 2 if dtype == BF16 else 4
        self.C, self.T, self.cb = C, T, T * esz
        self.arena = arena
        self.off = off
        if arena is None:
            self.t = P.sbuf(name, [128, C, T], dtype)
            self.ap = self.t[:]
            self.own = [Tk(f"{name}_{c}") for c in range(C)]
        else:
            assert off % 4 == 0 and off + C * T * esz <= arena.nbytes, (name, off, C, T)
            base = arena.tb if dtype == BF16 else (arena.ti if dtype == I32 else arena.t)
            e0 = off // esz
            self.ap = base[:, e0:e0 + C * T].rearrange("p (c t) -> p c t", t=T)

    def toks(self, c0, c1=None):
        c1 = c0 + 1 if c1 is None else c1
        if self.arena is None:
            return self.own[c0:c1]
        return self.arena.toks(self.off + c0 * self.cb, self.off + c1 * self.cb)

    def c(self, c, lo=None, hi=None, p0=0, p1=128):
        return V(self.ap[p0:p1, c, lo:hi], self.toks(c))

    def r(self, c0, c1, p0=0, p1=128):
        return V(self.ap[p0:p1, c0:c1, :], self.toks(c0, c1))

    def all(self):
        return self.r(0, self.C)


D = 1024
DFF = 2816
T = 512
NH = 32
HD = 64
NG = 4
NS = 128
DIN = 2048
MH = 8
QL = 512
KVL = 256
EPS = 1e-6
WIN_B = 8096
V_LN1, V_LNM, V_LN2, V_CW, V_CB, V_QLN, V_KVLN = 0, 8, 16, 24, 120, 144, 148
V_QN, V_QR, V_QRS, V_KN, V_KR, V_KRS, V_SSDN, NV = 150, 151, 152, 153, 154, 155, 156, 172
C_ID, C_U, C_INV, NCST = 0, 128, 256, 257
PI = 3.1415925
TWO_PI = 6.283185307179586
CW1 = 6.28125
CW2 = TWO_PI - 6.28125


def build(SEQ, DEPTH, dbg=None):
    NT = SEQ // T
    nc = bass.Bass("TRN2", target_bir_lowering=False)
    inp = lambda n, s, d=F32: nc.dram_tensor(n, list(s), d, kind="ExternalInput").ap()
    scr = lambda n, s, d=BF16: nc.dram_tensor(n, list(s), d, kind="Internal").ap()
    xT = inp("xT", [D, SEQ])
    pos = inp("pos", [1, SEQ], I32)
    cst = inp("cst", [128, NCST])
    oT = nc.dram_tensor("oT", [D, SEQ], F32, kind="ExternalOutput").ap()
    W = []
    for l in range(DEPTH):
        w = {}
        w["vecs"] = inp(f"vecs{l}", [128, NV])
        w["rows"] = inp(f"rows{l}", [1, 96])
        for nm, sh in [("ffn1_w13", [D, 2 * DFF]), ("ffn1_w2", [DFF, D]), ("w_in", [D, 8032]),
                       ("w_ssd_out", [DIN, D]), ("w_uq", [QL, 1536]), ("w_ukv", [KVL, 2048]),
                       ("w_mla_out", [D, D]), ("w_o", [D, D]), ("ffn2_w13", [D, 2 * DFF]), ("ffn2_w2", [DFF, D])]:
            w[nm] = inp(f"{nm}{l}", sh)
        for nm, sh in [("ffn1_w13", [D, 2 * DFF]), ("ffn1_w2", [DFF, D]), ("w_in", [D, WIN_B]),
                       ("w_ssd_out", [DIN, D]), ("w_uq", [QL, 2048]), ("w_ukv", [KVL, 2048]),
                       ("w_mla_out", [D, D]), ("w_o", [D, D]), ("ffn2_w13", [D, 2 * DFF]), ("ffn2_w2", [DFF, D])]:
            w[nm + "_b"] = scr(f"{nm}{l}_b", sh)
            w[nm + "_tk"] = Tk(f"{nm}{l}_b")
        W.append(w)
    hbuf = scr("hbuf", [D, SEQ], F32)
    KN = scr("KN", [MH, NT, 128, T])
    KR = scr("KR", [NT, 64, T])
    VV = scr("VV", [MH, NT, 128, 4, 128])
    dbg_out = None
    if dbg:
        dbg_out = nc.dram_tensor("dbg", [128, dbg], F32, kind="ExternalOutput").ap()

    st = ExitStack()
    with st:
        P = Prog(nc, st)

        def MM(o, l, r, start=True, stop=True, extraR=()):
            P.add("pe", lambda e: e.matmul(o.ap, lhsT=l.ap, rhs=r.ap, start=start, stop=stop),
                  R=l.tk + r.tk + list(extraR), W=o.tk, name="mm")

        def TR(o, i, ident):
            P.add("pe", lambda e: e.transpose(o.ap, i.ap, ident.ap), R=i.tk + ident.tk, W=o.tk)

        def ACT(o, i, func, bias=None, scale=None, accum=None, extraR=()):
            kw = {}
            R = i.tk + list(extraR)
            Wt = list(o.tk)
            if bias is not None:
                if isinstance(bias, V):
                    kw["bias"] = bias.ap
                    R += bias.tk
                else:
                    kw["bias"] = bias
            if scale is not None:
                if isinstance(scale, V):
                    kw["scale"] = scale.ap
                    R += scale.tk
                else:
                    kw["scale"] = scale
            if accum is not None:
                kw["accum_out"] = accum.ap
                Wt += accum.tk
            P.add("act", lambda e: e.activation(out=o.ap, in_=i.ap, func=func, **kw), R=R, W=Wt, name="act_" + str(func).split(".")[-1])

        def TT(eng, o, a, b, op):
            P.add(eng, lambda e: e.tensor_tensor(out=o.ap, in0=a.ap, in1=b.ap, op=op), R=a.tk + b.tk, W=o.tk)

        def TS(eng, o, a, s1, op0, s2=None, op1=None):
            R = list(a.tk)
            k1 = s1
            if isinstance(s1, V):
                R += s1.tk
                k1 = s1.ap
            k2 = s2
            if isinstance(s2, V):
                R += s2.tk
                k2 = s2.ap
            if op1 is None:
                P.add(eng, lambda e: e.tensor_scalar(out=o.ap, in0=a.ap, scalar1=k1, scalar2=None, op0=op0), R=R, W=o.tk)
            else:
                P.add(eng, lambda e: e.tensor_scalar(out=o.ap, in0=a.ap, scalar1=k1, scalar2=k2, op0=op0, op1=op1), R=R, W=o.tk)

        def STT(o, a, s, b, op0, op1):
            R = a.tk + b.tk
            k = s
            if isinstance(s, V):
                R += s.tk
                k = s.ap
            P.add("dve", lambda e: e.scalar_tensor_tensor(out=o.ap, in0=a.ap, scalar=k, in1=b.ap, op0=op0, op1=op1), R=R, W=o.tk)

        def CP(eng, o, i):
            if eng == "act":
                P.add("act", lambda e: e.activation(out=o.ap, in_=i.ap, func=AF.Copy), R=i.tk, W=o.tk, name="act_cp")
            else:
                P.add(eng, lambda e: e.tensor_copy(out=o.ap, in_=i.ap), R=i.tk, W=o.tk)

        def RCP(o, i):
            P.add("dve", lambda e: e.reciprocal(out=o.ap, in_=i.ap), R=i.tk, W=o.tk)

        def MEMSET(eng, o, val):
            P.add(eng, lambda e: e.memset(o.ap, val), W=o.tk)

        def DMA(eng, out_ap, in_ap, R=(), W=(), semtok=None, grp=None):
            return P.add(eng, lambda e: e.dma_start(out=out_ap, in_=in_ap), R=R, W=W, dma=True, semtok=semtok, grp=grp)

        def conv_w(src, dst, tk, pieces, rows, grp):
            for r0 in range(0, rows, 128):
                for (dc, sc, n) in pieces:
                    DMA("pool", dst[r0:r0 + 128, dc:dc + n], src[r0:r0 + 128, sc:sc + n], W=[tk], semtok=tk, grp=grp)

        win_pieces = [(0, 0, 5120), (5120, 5152, 512), (5632, 5664, 256), (5888, 5920, 64), (5952, 5952, 32),
                      (5984, 5920, 32), (6016, 5120, 32), (6048, 5984, 2048)]
        uq_pieces = []
        for h in range(MH):
            uq_pieces += [(128 * h, 192 * h, 128), (1024 + 64 * h, 192 * h + 128, 64),
                          (1536 + 64 * h, 192 * h + 160, 32), (1536 + 64 * h + 32, 192 * h + 128, 32)]
        ukv_pieces = []
        for h in range(MH):
            ukv_pieces += [(128 * h, 256 * h, 128), (1024 + 128 * h, 256 * h + 128, 128)]
        for l in range(DEPTH):
            w = W[l]
            order = [("ffn1_w13", [(0, 0, 2 * DFF)], D), ("ffn1_w2", [(0, 0, D)], DFF), ("w_in", win_pieces, D),
                     ("w_uq", uq_pieces, QL), ("w_ukv", ukv_pieces, KVL), ("w_mla_out", [(0, 0, D)], D),
                     ("w_ssd_out", [(0, 0, D)], DIN), ("w_o", [(0, 0, D)], D),
                     ("ffn2_w13", [(0, 0, 2 * DFF)], D), ("ffn2_w2", [(0, 0, D)], DFF)]
            for nm, pcs, rows in order:
                conv_w(w[nm], w[nm + "_b"], w[nm + "_tk"], pcs, rows, grp=f"cv_{nm}{l}")

        cstb = Buf(P, "cstb", 1, NCST, F32)
        DMA("sp", cstb.ap[:, 0, :], cst, W=cstb.toks(0), semtok=cstb.toks(0)[0])
        identb = Buf(P, "identb", 1, 128, BF16)
        Ub = Buf(P, "Ub", 1, 128, BF16)
        onesb = Buf(P, "onesb", 1, 128, BF16)
        onesf = Buf(P, "onesf", 1, 128, F32)
        ident_f = V(cstb.ap[:, 0, C_ID:C_ID + 128], cstb.toks(0))
        U_f = V(cstb.ap[:, 0, C_U:C_U + 128], cstb.toks(0))
        invf = V(cstb.ap[0:64, 0, C_INV:C_INV + 1], cstb.toks(0))
        CP("dve", identb.c(0), ident_f)
        CP("dve", Ub.c(0), U_f)
        MEMSET("dve", onesb.c(0), 1.0)
        MEMSET("dve", onesf.c(0), 1.0)
        ident_b = identb.c(0)
        epsb = Buf(P, "epsb", 1, 2, F32)
        MEMSET("dve", epsb.c(0), EPS)
        epsc = V(epsb.ap[:, 0, 0:1], epsb.toks(0))
        negm = Buf(P, "negm", 1, 512, BF16)
        for q_ in range(4):
            TS("dve", negm.c(0, q_ * 128, (q_ + 1) * 128), U_f, -1.0, ALU.add, 30000.0, ALU.mult)
        U_b = Ub.c(0)
        ones_b = onesb.c(0)
        ones_f = onesf.c(0)

        hT = Buf(P, "hT", 8, T, F32)
        xn = Buf(P, "xn", 8, T, BF16)
        mg = Buf(P, "mg", 8, T, BF16)
        Sst = Buf(P, "Sst", NG, 512, F32)
        Sb = Buf(P, "Sb", NG, 512, BF16)
        sck = Buf(P, "sck", NT * 4, 8, F32)
        halo = Buf(P, "halo", 24, 3, F32)
        vec = [Buf(P, f"vec{l}", 1, NV, F32) for l in range(DEPTH)]
        rowsb = [Buf(P, f"rowsb{l}", 1, 96, F32) for l in range(DEPTH)]
        aneg = [Buf(P, f"aneg{l}", 1, 32, F32) for l in range(DEPTH)]
        for l in range(DEPTH):
            DMA("sp", vec[l].ap[:, 0, :], W[l]["vecs"], W=vec[l].toks(0), semtok=vec[l].toks(0)[0])
            DMA("sp", rowsb[l].ap[:, 0, :], W[l]["rows"][0, :].partition_broadcast(128), W=rowsb[l].toks(0), semtok=rowsb[l].toks(0)[0])
            ACT(aneg[l].c(0), rowsb[l].c(0, 32, 64), AF.Exp)
            TS("dve", aneg[l].c(0), aneg[l].c(0), -1.0, ALU.mult)
        ws = WStream(P, "ws", [128, 8, 512], BF16, 4, 2)
        kvs = WStream(P, "kvs", [128, 1536], BF16, 4, 3)
        ps = [P.psum(f"ps{i}", [128, 512], F32) for i in range(8)]
        psb = [p.bitcast(BF16) for p in ps]
        tk_ps = [Tk(f"ps{i}") for i in range(8)]

        def PS(i, lo=0, hi=512, p0=0, p1=128):
            return V(ps[i][p0:p1, lo:hi], [tk_ps[i]])

        def PSB(i, lo=0, hi=1024, p0=0, p1=128):
            return V(psb[i][p0:p1, lo:hi], [tk_ps[i]])

        AR = Arena(P, "arena", 110 * 1024)
        KB = 1024
        sq = Buf(P, "sq", 2, T, BF16, AR, 0)
        rs = Buf(P, "rs", 1, T, F32, AR, 2 * KB)
        rstd = Buf(P, "rstd", 1, T, F32, AR, 4 * KB)
        sg = Buf(P, "sg", 2, T, F32, AR, 6 * KB)
        B0 = 10 * KB
        act = Buf(P, "act", 22, T, BF16, AR, B0)
        o = B0
        cqn = Buf(P, "cqn", 4, T, BF16, AR, o); o += 4 * KB
        ckvn = Buf(P, "ckvn", 2, T, BF16, AR, o); o += 2 * KB
        Qn = Buf(P, "Qn", 8, T, BF16, AR, o); o += 8 * KB
        Qr = Buf(P, "Qr", 8, T, BF16, AR, o); o += 8 * KB
        KTs = Buf(P, "KTs", 8, T, BF16, AR, o); o += 8 * KB
        KrT = Buf(P, "KrT", 1, T, BF16, AR, o); o += 1 * KB
        Vs = Buf(P, "Vs", 4, 1024, BF16, AR, o); o += 8 * KB
        PT = Buf(P, "PT", 3, T, BF16, AR, o); o += 3 * KB
        oTb = Buf(P, "oTb", 8, T, BF16, AR, o); o += 8 * KB
        cos2 = Buf(P, "cos2", 1, T, F32, AR, o); o += 2 * KB
        sinS = Buf(P, "sinS", 1, T, F32, AR, o); o += 2 * KB
        ra = Buf(P, "ra", 1, T, F32, AR, o); o += 2 * KB
        rb = Buf(P, "rb", 1, T, F32, AR, o); o += 2 * KB
        rstdq = Buf(P, "rstdq", 1, T, F32, AR, o); o += 2 * KB
        rd = Buf(P, "rd", 2, T, F32, AR, o); o += 4 * KB
        posi = Buf(P, "posi", 1, T, I32, AR, o); o += 2 * KB
        ksq = Buf(P, "ksq", 1, 128, F32, AR, o); o += 1 * KB
        mla_end = o
        o = B0
        zs = Buf(P, "zs", 4, DIN, BF16, AR, o); o += 16 * KB
        dAU = Buf(P, "dAU", 2, 1024, F32, AR, o)
        stg = Buf(P, "stg", 2, 516, F32, AR, o); o += 5 * KB
        cacc = Buf(P, "cacc", 2, T, F32, AR, o); o += 4 * KB
        xsT = Buf(P, "xsT", 16, T, BF16, AR, o); o += 16 * KB
        BT = Buf(P, "BT", 4, T, BF16, AR, o); o += 4 * KB
        CT = Buf(P, "CT", 4, T, BF16, AR, o); o += 4 * KB
        xdt = Buf(P, "xdt", 1, DIN, BF16, AR, o); o += 4 * KB
        xw = Buf(P, "xw", 1, DIN, BF16, AR, o); o += 4 * KB
        xtok = Buf(P, "xtok", 1, DIN, BF16, AR, o); o += 4 * KB
        Btok = Buf(P, "Btok", 1, 512, BF16, AR, o); o += 1 * KB
        Eb = Buf(P, "Eb", 2, 1024, BF16, AR, o); o += 4 * KB
        Mt = Buf(P, "Mt", 2, 1024, BF16, AR, o); o += 4 * KB
        CBm = Buf(P, "CBm", 1, 512, F32, AR, o); o += 2 * KB
        yb = Buf(P, "yb", 1, DIN, F32, AR, o); o += 8 * KB
        ynb = Buf(P, "ynb", 1, DIN, BF16, AR, o); o += 4 * KB
        tmpa = Buf(P, "tmpa", 2, 512, F32, AR, o); o += 4 * KB
        tmpb = Buf(P, "tmpb", 2, 512, F32, AR, o); o += 4 * KB
        dts = Buf(P, "dts", 8, 128, F32, AR, o); o += 4 * KB
        ksq2 = Buf(P, "ksq2", 1, 8, F32, AR, o); o += 1 * KB
        ssd_end = o
        assert mla_end <= AR.nbytes and ssd_end <= AR.nbytes, (mla_end, ssd_end)
        ynT = xsT

        xTv = xT.rearrange("(c p) t -> p c t", p=128)
        oTv = oT.rearrange("(c p) t -> p c t", p=128)
        hbv = hbuf.rearrange("(c p) t -> p c t", p=128)
        tk_hb = [Tk(f"hb{i}") for i in range(NT)]
        tk_kvc = [Tk(f"kvc{i}") for i in range(NT)]
        kv_store_pos = {}
        out_ops = []

        def wslab(w, nm, kc0, kc1, c0, ncols, dcol=0):
            src = w[nm + "_b"].rearrange("(c p) n -> p c n", p=128)[:, kc0:kc1, c0:c0 + ncols]
            n = kc1 - kc0
            return (lambda b: b[:, 0:n, dcol:dcol + ncols], src)

        def rmsnorm_to_xn(gcol, l):
            for c in range(8):
                ACT(sq.c(c % 2), hT.c(c), AF.Square)
                MM(PS(7), ones_b, sq.c(c % 2), start=(c == 0), stop=(c == 7))
            ACT(rs.c(0), PS(7), AF.Sqrt, bias=EPS, scale=1.0 / D)
            RCP(rstd.c(0), rs.c(0))
            for c in range(8):
                STT(xn.c(c), hT.c(c), V(vec[l].ap[:, 0, gcol + c:gcol + c + 1], vec[l].toks(0)), rstd.c(0), ALU.mult, ALU.mult)

        def ffn(l, which):
            w = W[l]
            n13, n2 = f"ffn{which}_w13", f"ffn{which}_w2"
            rmsnorm_to_xn(V_LN1 if which == 1 else V_LN2, l)
            for j in range(11):
                c0 = j * 256
                buf, tk = ws.get([wslab(w, n13, 0, 8, c0, 256, 0), wslab(w, n13, 0, 8, DFF + c0, 256, 256)], extraR=[w[n13 + "_tk"]])
                pb = 0 if j % 2 == 0 else 4
                for q in range(4):
                    for k in range(8):
                        MM(PS(pb + q), V(buf[:, k, q * 128:(q + 1) * 128], [tk]), xn.c(k), start=(k == 0), stop=(k == 7))
                for q in range(2):
                    b = (2 * j + q) % 2
                    ACT(sg.c(b), PS(pb + q), AF.Silu)
                    TT("dve", act.c(2 * j + q), sg.c(b), PS(pb + q + 2), ALU.mult)
            for half in range(2):
                pb = 0 if half == 0 else 4
                for (k0, k1) in [(0, 8), (8, 16), (16, 22)]:
                    buf, tk = ws.get([wslab(w, n2, k0, k1, half * 512, 512)], extraR=[w[n2 + "_tk"]])
                    for k in range(k0, k1):
                        for q in range(4):
                            MM(PS(pb + q), V(buf[:, k - k0, q * 128:(q + 1) * 128], [tk]), act.c(k), start=(k == 0), stop=(k == 21))
                for q in range(4):
                    c = half * 4 + q
                    STT(hT.c(c), PS(pb + q), 0.5, hT.c(c), ALU.mult, ALU.add)

        def vcol(l, col, p0=0, p1=128):
            return V(vec[l].ap[p0:p1, 0, col:col + 1], vec[l].toks(0))

        def rope_tables(t0):
            DMA("sp", posi.ap[0:64, 0, :], pos[0, t0:t0 + T].partition_broadcast(64), W=posi.toks(0), semtok=posi.toks(0)[0])
            ang = V(ra.ap[0:64, 0, :], ra.toks(0))
            kk = V(rb.ap[0:64, 0, :], rb.toks(0))
            CP("dve", ang, V(posi.ap[0:64, 0, :], posi.toks(0)))
            TS("dve", ang, ang, invf, ALU.mult)
            for (dst, shift) in ((sinS, 0.0), (cos2, math.pi / 2)):
                d = V(dst.ap[0:64, 0, :], dst.toks(0))
                TS("dve", kk, ang, 1.0 / TWO_PI, ALU.mult, shift / TWO_PI, ALU.add)
                TS("dve", kk, kk, 12582912.0, ALU.add)
                TS("dve", kk, kk, -12582912.0, ALU.add)
                STT(d, kk, -CW1, ang, ALU.mult, ALU.add)
                STT(d, kk, -CW2, d, ALU.mult, ALU.add)
                TS("dve", d, d, shift, ALU.add, -PI, ALU.max)
                TS("dve", d, d, PI, ALU.min)
                ACT(d, d, AF.Sin)
            TS("dve", V(sinS.ap[0:32, 0, :], sinS.toks(0)), V(sinS.ap[0:32, 0, :], sinS.toks(0)), -1.0, ALU.mult)

        def mla(l, it):
            w = W[l]
            t0 = it * T
            rope_tables(t0)
            cs = V(cos2.ap[0:64, 0, :], cos2.toks(0))
            sn = V(sinS.ap[0:64, 0, :], sinS.toks(0))
            wt = [w["w_in_tk"]]
            if KMLA < 1:
                return []
            buf, tk = ws.get([wslab(w, "w_in", 0, 8, 5120, 512)], extraR=wt)
            for q in range(4):
                for k in range(8):
                    MM(PS(q), V(buf[:, k, q * 128:(q + 1) * 128], [tk]), xn.c(k), start=(k == 0), stop=(k == 7))
            for q in range(4):
                if KV >= 2:
                    ACT(sq.c(q % 2), PS(q), AF.Square)
                    MM(PS(7), ones_b, sq.c(q % 2), start=(q == 0), stop=(q == 3))
                if KV >= 3:
                    ACT(cqn.c(q), PS(q), AF.Copy, scale=vcol(l, V_QLN + q))
            if KV >= 2:
                ACT(rs.c(0), PS(7), AF.Sqrt, bias=EPS, scale=1.0 / QL)
                RCP(rstd.c(0), rs.c(0))
            if KV >= 4:
                for q in range(4):
                    TT("dve", cqn.c(q), cqn.c(q), rstd.c(0), ALU.mult)
            if KMLA < 2:
                return []
            buf, tk = ws.get([wslab(w, "w_in", 0, 8, 5632, 384)], extraR=wt)
            for q in range(2):
                for k in range(8):
                    MM(PS(4 + q), V(buf[:, k, q * 128:(q + 1) * 128], [tk]), xn.c(k), start=(k == 0), stop=(k == 7))
            for k in range(8):
                MM(PS(6, 0, 512, 0, 64), V(buf[:, k, 256:320], [tk]), xn.c(k), start=(k == 0), stop=(k == 7))
            for k in range(8):
                MM(PS(3, 0, 512, 0, 64), V(buf[:, k, 320:384], [tk]), xn.c(k), start=(k == 0), stop=(k == 7))
            for q in range(2):
                ACT(sq.c(q % 2), PS(4 + q), AF.Square)
                MM(PS(7), ones_b, sq.c(q % 2), start=(q == 0), stop=(q == 1))
                ACT(ckvn.c(q), PS(4 + q), AF.Copy, scale=vcol(l, V_KVLN + q))
            ACT(rs.c(0), PS(7), AF.Sqrt, bias=EPS, scale=1.0 / KVL)
            RCP(rstd.c(0), rs.c(0))
            for q in range(2):
                TT("dve", ckvn.c(q), ckvn.c(q), rstd.c(0), ALU.mult)
            a64 = V(ra.ap[0:64, 0, :], ra.toks(0))
            b64 = V(rb.ap[0:64, 0, :], rb.toks(0))
            ACT(V(sq.ap[0:64, 0, :], sq.toks(0)), PS(6, 0, 512, 0, 64), AF.Square)
            ACT(a64, PS(6, 0, 512, 0, 64), AF.Copy, scale=vcol(l, V_KR, 0, 64))
            TT("dve", a64, a64, cs, ALU.mult)
            ACT(b64, PS(3, 0, 512, 0, 64), AF.Copy, scale=vcol(l, V_KRS, 0, 64))
            TT("dve", b64, b64, sn, ALU.mult)
            TT("dve", V(KrT.ap[0:64, 0, :], KrT.toks(0)), a64, b64, ALU.add)
            for cc in range(4):
                MM(PS(7, cc * 9 + 8, cc * 9 + 9), V(sq.ap[0:64, 0, cc * 128:(cc + 1) * 128], sq.toks(0)),
                   V(onesb.ap[0:64, 0, 0:1], onesb.toks(0)))
            if KMLA < 3:
                return []
            wt2 = [w["w_ukv_tk"]]
            for half in range(2):
                buf, tk = ws.get([wslab(w, "w_ukv", 0, 2, half * 512, 512)], extraR=wt2)
                for q in range(4):
                    h = half * 4 + q
                    pb = q % 2
                    for k in range(2):
                        MM(PS(pb), V(buf[:, k, q * 128:(q + 1) * 128], [tk]), ckvn.c(k), start=(k == 0), stop=(k == 1))
                    ACT(sq.c(pb), PS(pb), AF.Square)
                    for cc in range(4):
                        MM(PS(7, cc * 9 + h, cc * 9 + h + 1), sq.c(pb, cc * 128, (cc + 1) * 128), V(onesb.ap[:, 0, 0:1], onesb.toks(0)))
                    ACT(KTs.c(h), PS(pb), AF.Copy, scale=vcol(l, V_KN))
            CP("act", V(ksq.ap[:, 0, 64:100], ksq.toks(0)), PS(7, 0, 36))
            for cc in range(4):
                kq = V(ksq.ap[:, 0, cc * 8:(cc + 1) * 8], ksq.toks(0))
                TS("dve", kq, V(ksq.ap[:, 0, 64 + cc * 9:64 + cc * 9 + 8], ksq.toks(0)),
                   V(ksq.ap[:, 0, 64 + cc * 9 + 8:64 + cc * 9 + 9], ksq.toks(0)), ALU.add)
                ACT(kq, kq, AF.Sqrt, bias=EPS * 192.0, scale=1.0)
                RCP(sck.c(it * 4 + cc), kq)
            if KMLA < 4:
                return []
            for half in range(2):
                buf, tk = ws.get([wslab(w, "w_ukv", 0, 2, 1024 + half * 512, 512)], extraR=wt2)
                for cc in range(4):
                    pb = 2 + cc
                    for k in range(2):
                        MM(PS(pb), ckvn.c(k, cc * 128, (cc + 1) * 128), V(buf[:, k, :], [tk]), start=(k == 0), stop=(k == 1))
                    CP("act", Vs.c(cc, half * 512, (half + 1) * 512), PS(pb))
            if KMLA < 5:
                return []
            st_ops = []
            for h in range(MH):
                st_ops.append(DMA("pool", KN[h, it], KTs.ap[:, h, :], R=KTs.toks(h), W=[tk_kvc[it]], semtok=tk_kvc[it], grp=("kvst", l, it)))
                st_ops.append(DMA("pool", VV[h, it], Vs.ap[:, :, h * 128:(h + 1) * 128], R=Vs.toks(0, 4), W=[tk_kvc[it]], semtok=tk_kvc[it], grp=("kvst", l, it)))
            st_ops.append(DMA("pool", KR[it], KrT.ap[0:64, 0, :], R=KrT.toks(0), W=[tk_kvc[it]], semtok=tk_kvc[it], grp=("kvst", l, it)))
            kv_store_pos[it] = len(P.ops)
            if KMLA < 6:
                return []
            wq = [w["w_uq_tk"]]
            slabs = {}
            for h in range(MH):
                if h % 4 == 0:
                    bn, tn = ws.get([wslab(w, "w_uq", 0, 4, (h // 4) * 512, 512)], extraR=wq)
                if h % 8 == 0:
                    br, tr = ws.get([wslab(w, "w_uq", 0, 4, 1024, 512)], extraR=wq)
                    bs, ts_ = ws.get([wslab(w, "w_uq", 0, 4, 1536, 512)], extraR=wq)
                hq = h % 4
                for k in range(4):
                    MM(PS(0), V(bn[:, k, hq * 128:(hq + 1) * 128], [tn]), cqn.c(k), start=(k == 0), stop=(k == 3))
                for k in range(4):
                    MM(PS(1, 0, 512, 0, 64), V(br[:, k, h * 64:(h + 1) * 64], [tr]), cqn.c(k), start=(k == 0), stop=(k == 3))
                for k in range(4):
                    MM(PS(2, 0, 512, 0, 64), V(bs[:, k, h * 64:(h + 1) * 64], [ts_]), cqn.c(k), start=(k == 0), stop=(k == 3))
                ACT(sq.c(0), PS(0), AF.Square)
                ACT(V(sq.ap[0:64, 1, :], sq.toks(1)), PS(1, 0, 512, 0, 64), AF.Square)
                MM(PS(3), ones_b, sq.c(0), start=True, stop=False)
                MM(PS(3), V(onesb.ap[0:64, 0, :], onesb.toks(0)), V(sq.ap[0:64, 1, :], sq.toks(1)), start=False, stop=True)
                ACT(rs.c(0), PS(3), AF.Sqrt, bias=EPS, scale=1.0 / 192.0)
                RCP(rstdq.c(0), rs.c(0))
                ACT(sg.c(0), PS(0), AF.Copy, scale=vcol(l, V_QN))
                TT("dve", Qn.c(h), sg.c(0), rstdq.c(0), ALU.mult)
                ACT(a64, PS(1, 0, 512, 0, 64), AF.Copy, scale=vcol(l, V_QR, 0, 64))
                TT("dve", a64, a64, cs, ALU.mult)
                ACT(b64, PS(2, 0, 512, 0, 64), AF.Copy, scale=vcol(l, V_QRS, 0, 64))
                TT("dve", b64, b64, sn, ALU.mult)
                TT("pool", a64, a64, b64, ALU.add)
                TT("pool", V(Qr.ap[0:64, h, :], Qr.toks(h)), a64, V(rstdq.ap[0:64, 0, :], rstdq.toks(0)), ALU.mult)
            if KMLA < 7:
                return []
            nokv = bool(_os.environ.get("KNOKV"))
            for h in range(MH):
                po, pd = (0, 1) if h % 2 == 0 else (5, 6)
                blocks = [(j, b) for j in range(it + 1) for b in range(4)]
                nb_ = len(blocks)
                ops_ = {}

                def S_(n):
                    j, b = blocks[n]
                    if j < it and not nokv and b == 0:
                        kb, kt = kvs.get([
                            (lambda b_: b_[:, 0:512], KN[h, j]),
                            (lambda b_: b_[0:64, 512:1024], KR[j]),
                            (lambda b_: b_[:, 1024:1536].rearrange("p (c d) -> p c d", d=128), VV[h, j]),
                        ], extraR=[tk_kvc[j]], minpos=kv_store_pos[j])
                        ops_[j] = (kb, kt)
                    q0 = 128 * b if j == it else 0
                    si = 2 + (n % 3)
                    pt = n % 3
                    if j < it and not nokv:
                        kb, kt = ops_[j]
                        kn_ = V(kb[:, b * 128:(b + 1) * 128], [kt])
                        kr_ = V(kb[0:64, 512 + b * 128:512 + (b + 1) * 128], [kt])
                    else:
                        kn_ = KTs.c(h, b * 128, (b + 1) * 128)
                        kr_ = V(KrT.ap[0:64, 0, b * 128:(b + 1) * 128], KrT.toks(0))
                    MM(PS(si, q0, 512), kn_, Qn.c(h, q0, 512), start=True, stop=False)
                    MM(PS(si, q0, 512), kr_, V(Qr.ap[0:64, h, q0:512], Qr.toks(h)), start=False, stop=True)
                    ACT(PT.c(pt, q0, 512), PS(si, q0, 512), AF.Exp, scale=V(sck.ap[:, j * 4 + b, h:h + 1], sck.toks(j * 4 + b)))
                    if j == it:
                        TT("pool", PT.c(pt, q0, q0 + 128), PT.c(pt, q0, q0 + 128), U_b, ALU.mult)

                def PV_(n):
                    j, b = blocks[n]
                    q0 = 128 * b if j == it else 0
                    pt = n % 3
                    if j < it and not nokv:
                        kb, kt = ops_[j]
                        v_ = V(kb[:, 1024 + b * 128:1024 + (b + 1) * 128], [kt])
                    else:
                        v_ = Vs.c(b, h * 128, (h + 1) * 128)
                    MM(PS(po, q0, 512), v_, PT.c(pt, q0, 512), start=(n == 0), stop=(n == nb_ - 1))
                    MM(PS(pd, q0, 512), ones_b, PT.c(pt, q0, 512), start=(n == 0), stop=(n == nb_ - 1))

                for n in range(nb_ + 2):
                    if n < nb_:
                        S_(n)
                    if n >= 2:
                        PV_(n - 2)
                RCP(rd.c(h % 2), PS(pd))
                TT("dve", oTb.c(h), PS(po), rd.c(h % 2), ALU.mult)
            for half in range(2):
                buf, tk = ws.get([wslab(w, "w_mla_out", 0, 8, half * 512, 512)], extraR=[w["w_mla_out_tk"]])
                for q in range(4):
                    for k in range(8):
                        MM(PS(q), V(buf[:, k, q * 128:(q + 1) * 128], [tk]), oTb.c(k), start=(k == 0), stop=(k == 7))
                buf, tk = ws.get([wslab(w, "w_in", 0, 8, 6048 + 1024 + half * 512, 512)], extraR=wt)
                for q in range(4):
                    for k in range(8):
                        MM(PS(4 + q), V(buf[:, k, q * 128:(q + 1) * 128], [tk]), xn.c(k), start=(k == 0), stop=(k == 7))
                for q in range(4):
                    ACT(sg.c(q % 2), PS(4 + q), AF.Sigmoid)
                    TT("dve", mg.c(half * 4 + q), sg.c(q % 2), PS(q), ALU.mult)
            return st_ops

        ssd_start = [0]

        def ssd(l, it):
            w = W[l]
            ssd_start[0] = len(P.ops)
            wt = [w["w_in_tk"]]
            if KV == 8:
                ACT(V(rs.ap[:, 0, 0:32], rs.toks(0)), V(rs.ap[:, 0, 32:64], rs.toks(0)), AF.Copy)
            if KV == 9:
                ACT(V(rs.ap[:, 0, 0:32], rs.toks(0)), V(rs.ap[:, 0, 32:64], rs.toks(0)), AF.Copy)
                ACT(V(rs.ap[:, 0, 0:32], rs.toks(0)), V(rs.ap[:, 0, 32:64], rs.toks(0)), AF.Copy)
            for nb in [int(c_) for c_ in _os.environ.get("KNB", "0123")]:
                buf, tk = ws.get([wslab(w, "w_in", 0, 8, nb * 512, 512)], extraR=wt)
                for cc in range(4):
                    pb = (nb % 2) * 4 + cc
                    for k in range(8):
                        MM(PS(pb), xn.c(k, cc * 128, (cc + 1) * 128), V(buf[:, k, :], [tk]), start=(k == 0), stop=(k == 7))
                    ACT(zs.c(cc, nb * 512, (nb + 1) * 512), PS(pb), AF.Silu)
            def cstage1(ch, buf, tk):
                q = ch % 4
                sb = ch // 4
                pb = (sb % 2) * 4 + q
                for k in range(8):
                    MM(PS(pb), V(buf[:, k, q * 128:(q + 1) * 128], [tk]), xn.c(k), start=(k == 0), stop=(k == 7))
                s_ = ch % 2
                CP("act", stg.c(s_, 3, 515), PS(pb))
                CP("pool", stg.c(s_, 0, 3), halo.c(ch))

            def cstage2(ch):
                s_ = ch % 2
                ca = cacc.c(s_)
                TS("dve", ca, stg.c(s_, 3, 515), vcol(l, V_CW + ch * 4 + 3), ALU.mult)
                STT(ca, stg.c(s_, 2, 514), vcol(l, V_CW + ch * 4 + 2), ca, ALU.mult, ALU.add)
                STT(ca, stg.c(s_, 1, 513), vcol(l, V_CW + ch * 4 + 1), ca, ALU.mult, ALU.add)
                STT(ca, stg.c(s_, 0, 512), vcol(l, V_CW + ch * 4 + 0), ca, ALU.mult, ALU.add)
                CP("pool", halo.c(ch), stg.c(s_, 512, 515))
                dst = xsT.c(ch) if ch < 16 else (BT.c(ch - 16) if ch < 20 else CT.c(ch - 20))
                ACT(dst, ca, AF.Silu, bias=vcol(l, V_CB + ch))

            cur = None
            for ch in range(24):
                if ch % 4 == 0:
                    cur = ws.get([wslab(w, "w_in", 0, 8, 2048 + (ch // 4) * 512, 512)], extraR=wt)
                cstage1(ch, cur[0], cur[1])
                if ch >= 1:
                    cstage2(ch - 1)
            cstage2(23)
            bufd, tkd = ws.get([wslab(w, "w_in", 0, 8, 6016, 32)], extraR=wt)
            rw = rowsb[l]
            dtb3 = V(rw.ap[:, 0, 0:32].unsqueeze(1).broadcast_to([128, 4, 32]), rw.toks(0))
            an3 = V(aneg[l].ap[:, 0, :].unsqueeze(1).broadcast_to([128, 4, 32]), aneg[l].toks(0))
            Dv = V(rw.ap[:, 0, 64:96], rw.toks(0))
            for cc in range(4):
                for k in range(8):
                    MM(PS(7, cc * 32, cc * 32 + 32), xn.c(k, cc * 128, (cc + 1) * 128), V(bufd[:, k, 0:32], [tkd]), start=(k == 0), stop=(k == 7))
            dtv, dA, nac, eac, wv, ela, dtw, tmp8 = [dts.c(i) for i in range(8)]
            v3 = lambda v_: V(v_.ap.rearrange("p (c h) -> p c h", h=32), v_.tk)
            TT("dve", v3(dtv), v3(PS(7, 0, 128)), dtb3, ALU.add)
            ACT(dtv, dtv, AF.Exp)
            ACT(dtv, dtv, AF.Ln, bias=1.0)
            TT("dve", v3(dA), v3(dtv), an3, ALU.mult)
            for cc in range(4):
                MM(PS(7, 128 + cc * 32, 160 + cc * 32), U_f, V(dA.ap[:, cc * 32:(cc + 1) * 32], dA.tk))
                MM(PS(7, 256 + cc * 32, 288 + cc * 32), ones_f, V(dA.ap[:, cc * 32:(cc + 1) * 32], dA.tk))
            TS("dve", nac, PS(7, 128, 256), -1.0, ALU.mult)
            ACT(eac, PS(7, 128, 256), AF.Exp)
            TT("dve", tmp8, nac, PS(7, 256, 384), ALU.add)
            ACT(wv, tmp8, AF.Exp)
            ACT(ela, PS(7, 256, 384), AF.Exp)
            TT("dve", dtw, dtv, wv, ALU.mult)
            r3 = lambda v_: V(v_.ap.rearrange("p (h d) -> p h d", d=64), v_.tk)
            ssd0 = ssd_start[0]
            if _os.environ.get("KMARK"):
                print("MARK dt-hoist end", len(P.ops) - ssd0)
            for cc in range(4):
                if _os.environ.get("KMARK"):
                    print("MARK chunk", cc, len(P.ops) - ssd0)
                csl = slice(cc * 32, (cc + 1) * 32)
                dtv_c = V(dtv.ap[:, csl], dtv.tk)
                dtw_c = V(dtw.ap[:, csl], dtw.tk)
                dA_c = V(dA.ap[:, csl], dA.tk)
                nac_c = V(nac.ap[:, csl], nac.tk)
                eac_c = V(eac.ap[:, csl], eac.tk)
                ela_c = V(ela.ap[:, csl], ela.tk)
                for f in range(16):
                    TR(PSB(f // 8, (f % 8) * 128, (f % 8 + 1) * 128), xsT.c(f, cc * 128, (cc + 1) * 128), ident_b)
                for g in range(NG):
                    TR(PSB(2, g * 128, (g + 1) * 128), BT.c(g, cc * 128, (cc + 1) * 128), ident_b)
                for hb in range(2):
                    src = V(psb[hb][:, :].rearrange("p (h d) -> p h d", d=64), [tk_ps[hb]])
                    hs = slice(hb * 16, (hb + 1) * 16)
                    cs_ = slice(hb * 1024, (hb + 1) * 1024)
                    bc = lambda b_: V(b_.ap[:, hs].unsqueeze(2).broadcast_to([128, 16, 64]), b_.tk)
                    o3 = lambda b_: V(b_.ap[:, 0, cs_].rearrange("p (h d) -> p h d", d=64), b_.toks(0))
                    TT("dve", o3(xdt), src, bc(dtv_c), ALU.mult)
                    TT("dve", o3(xw), src, bc(dtw_c), ALU.mult)
                    TT("dve", o3(xtok), src, bc(V(Dv.ap, Dv.tk)), ALU.mult)
                CP("dve", Btok.c(0), PSB(2, 0, 512))
                for g in range(NG):
                    MM(PS(3, g * 128, (g + 1) * 128), BT.c(g, cc * 128, (cc + 1) * 128), CT.c(g, cc * 128, (cc + 1) * 128))
                CP("act", CBm.c(0), PS(3))

                def front(g):
                    e_ = g % 2
                    ab = (4, 5) if e_ == 0 else (0, 1)
                    U3 = V(cstb.ap[:, 0, C_U:C_U + 128].unsqueeze(1).broadcast_to([128, 8, 128]), cstb.toks(0))
                    dA3 = V(dA_c.ap[:, g * 8:(g + 1) * 8].unsqueeze(2).broadcast_to([128, 8, 128]), dA_c.tk)
                    TT("pool", V(dAU.ap[:, e_, :].rearrange("p (h t) -> p h t", t=128), dAU.toks(e_)), U3, dA3, ALU.mult)
                    for q in range(2):
                        MM(PS(ab[q]), ones_f, dAU.c(e_, q * 512, (q + 1) * 512), start=True, stop=False)
                        MM(PS(ab[q]), ident_b, negm.c(0), start=False, stop=True)
                    for hh in range(8):
                        h = g * 8 + hh
                        ACT(Eb.c(e_, hh * 128, (hh + 1) * 128), PS(ab[hh // 4], (hh % 4) * 128, (hh % 4 + 1) * 128), AF.Exp,
                            bias=V(nac_c.ap[:, h:h + 1], nac_c.tk))
                    E3 = V(Eb.ap[:, e_, :].rearrange("p (h t) -> p h t", t=128), Eb.toks(e_))
                    M3 = V(Mt.ap[:, e_, :].rearrange("p (h t) -> p h t", t=128), Mt.toks(e_))
                    CB3 = V(CBm.ap[:, 0, g * 128:(g + 1) * 128].unsqueeze(1).broadcast_to([128, 8, 128]), CBm.toks(0))
                    TT("dve", M3, E3, CB3, ALU.mult)

                def back(g):
                    e_ = g % 2
                    for hh in range(8):
                        h = g * 8 + hh
                        if hh == 0:
                            MM(PS(6), ident_b, xtok.c(0, g * 512, (g + 1) * 512), start=True, stop=False)
                        MM(PS(6, hh * 64, (hh + 1) * 64), Mt.c(e_, hh * 128, (hh + 1) * 128), xdt.c(0, h * 64, (h + 1) * 64),
                           start=False, stop=(hh == 7))
                    MM(PS(2), CT.c(g, cc * 128, (cc + 1) * 128), Sb.c(g))
                    eb3 = V(eac_c.ap[:, g * 8:(g + 1) * 8].unsqueeze(2).broadcast_to([128, 8, 64]), eac_c.tk)
                    TT("dve", r3(tmpa.c(e_)), r3(PS(2)), eb3, ALU.mult)
                    TT("dve", yb.c(0, g * 512, (g + 1) * 512), PS(6), tmpa.c(e_), ALU.add)

                def backS(g):
                    pbk = 3 if g % 2 == 0 else 7
                    MM(PS(pbk), Btok.c(0, g * 128, (g + 1) * 128), xw.c(0, g * 512, (g + 1) * 512))
                    el3 = V(ela_c.ap[:, g * 8:(g + 1) * 8].unsqueeze(2).broadcast_to([128, 8, 64]), ela_c.tk)
                    TT("pool", r3(Sst.c(g)), r3(Sst.c(g)), el3, ALU.mult)
                    TT("dve", Sst.c(g), Sst.c(g), PS(pbk), ALU.add)
                    CP("pool", Sb.c(g), Sst.c(g))

                front(0)
                front(1)
                back(0)
                front(2)
                back(1)
                front(3)
                back(2)
                back(3)
                for g in range(NG):
                    backS(g)
                TT("dve", yb.c(0), yb.c(0), zs.c(cc), ALU.mult)
                ssqy = V(ksq2.ap[:, 0, 0:1], ksq2.toks(0))
                ACT(ynb.c(0), yb.c(0), AF.Square, accum=ssqy)
                ACT(ssqy, ssqy, AF.Ln, bias=epsc, scale=1.0 / DIN)
                ACT(V(ksq2.ap[:, 0, 1:2], ksq2.toks(0)), ssqy, AF.Exp, scale=-0.5)
                ACT(ynb.c(0), yb.c(0), AF.Copy, scale=V(ksq2.ap[:, 0, 1:2], ksq2.toks(0)))
                for f in range(16):
                    TR(PSB(f // 8, (f % 8) * 128, (f % 8 + 1) * 128), ynb.c(0, f * 128, (f + 1) * 128), ident_b)
                for hb in range(2):
                    CP("dve", V(ynT.ap[:, hb * 8:(hb + 1) * 8, cc * 128:(cc + 1) * 128], ynT.toks(hb * 8, hb * 8 + 8)),
                       V(psb[hb][:, :].rearrange("p (f t) -> p f t", t=128), [tk_ps[hb]]))
            for f in range(16):
                TS("pool", ynT.c(f), ynT.c(f), vcol(l, V_SSDN + f), ALU.mult, 1.0, ALU.mult)
            if _os.environ.get("KMARK"):
                print("MARK outproj", len(P.ops) - ssd0)
            for half in range(2):
                for kg in range(2):
                    buf, tk = ws.get([wslab(w, "w_ssd_out", kg * 8, kg * 8 + 8, half * 512, 512)], extraR=[w["w_ssd_out_tk"]])
                    for k in range(8):
                        for q in range(4):
                            MM(PS(q), V(buf[:, k, q * 128:(q + 1) * 128], [tk]), ynT.c(kg * 8 + k), start=(kg == 0 and k == 0), stop=(kg == 1 and k == 7))
                buf, tk = ws.get([wslab(w, "w_in", 0, 8, 6048 + half * 512, 512)], extraR=wt)
                for q in range(4):
                    for k in range(8):
                        MM(PS(4 + q), V(buf[:, k, q * 128:(q + 1) * 128], [tk]), xn.c(k), start=(k == 0), stop=(k == 7))
                for q in range(4):
                    c = half * 4 + q
                    ACT(sg.c(q % 2), PS(4 + q), AF.Sigmoid)
                    TT("dve", sg.c(q % 2), sg.c(q % 2), PS(q), ALU.mult)
                    TT("pool", mg.c(c), mg.c(c), sg.c(q % 2), ALU.add)
            for half in range(2):
                buf, tk = ws.get([wslab(w, "w_o", 0, 8, half * 512, 512)], extraR=[w["w_o_tk"]])
                pb = half * 4
                for q in range(4):
                    for k in range(8):
                        MM(PS(pb + q), V(buf[:, k, q * 128:(q + 1) * 128], [tk]), mg.c(k), start=(k == 0), stop=(k == 7))
                for q in range(4):
                    c = half * 4 + q
                    TT("dve", hT.c(c), hT.c(c), PS(pb + q), ALU.add)

        dbg_col = [0]

        def dump(v, n):
            if dbg_out is None:
                return
            c0 = dbg_col[0]
            dbg_col[0] += n
            out_ops.append(DMA("pool", dbg_out[0:v.ap.shape[0], c0:c0 + n], v.ap, R=v.tk, semtok=Tk("dbgs")))

        for l in range(DEPTH):
            for g in range(NG):
                MEMSET("pool", Sst.c(g), 0.0)
                MEMSET("pool", Sb.c(g), 0.0)
            MEMSET("pool", halo.all(), 0.0)
            for it in range(NT):
                t0 = it * T
                if l == 0:
                    DMA("sp", hT.ap, xTv[:, :, t0:t0 + T], W=hT.toks(0, 8), semtok=hT.toks(0)[0])
                else:
                    DMA("sp", hT.ap, hbv[:, :, t0:t0 + T], R=[tk_hb[it]], W=hT.toks(0, 8), semtok=hT.toks(0)[0])
                if "f1" in PHASES:
                    ffn(l, 1)
                if "nm" in PHASES:
                    rmsnorm_to_xn(V_LNM, l)
                if "mla" in PHASES:
                    mla(l, it)
                if "ssd" in PHASES:
                    if KSSD < 999999:
                        P.limit = len(P.ops) + KSSD
                    ssd(l, it)
                    P.limit = None
                if "f2" in PHASES:
                    ffn(l, 2)
                if l == DEPTH - 1:
                    out_ops.append(DMA("pool", oTv[:, :, t0:t0 + T], hT.ap, R=hT.toks(0, 8), semtok=hT.toks(1)[0]))
                else:
                    DMA("pool", hbv[:, :, t0:t0 + T], hT.ap, R=hT.toks(0, 8), W=[tk_hb[it]], semtok=hT.toks(1)[0])
        fence = P.add("pool", None)
        ws.flush()
        kvs.flush()
        P.finalize()
        lastdma = {}
        for op_ in P.final:
            if op_.dma:
                k_ = id(op_.sem)
                if k_ not in lastdma or lastdma[k_].cnt <= op_.cnt:
                    lastdma[k_] = op_
        fence.deps = list(out_ops) + list(lastdma.values())
        if _os.environ.get("KDUMP"):
            n0 = int(_os.environ["KDUMP"])
            idx = {id(op): i for i, op in enumerate(P.final)}
            for i, op in enumerate(P.final[-n0:]):
                ds = [(idx.get(id(d), -1), d.eng, d.name, d.cnt) for d in op.deps]
                print(len(P.final) - n0 + i, op.eng, op.name, "dma" if op.dma else "", "sig" if op.sig else "", op.cnt,
                      [t.name for t in op.W][:3], "<-", ds)
        P.emit()
        build.stats = (len(P.final), P.nsem)
    return nc


def _consts():
    c = np.zeros((128, NCST), np.float32)
    c[:, C_ID:C_ID + 128] = np.eye(128, dtype=np.float32)
    c[:, C_U:C_U + 128] = np.triu(np.ones((128, 128), np.float32))
    inv = (1.0 / (np.float32(10000.0) ** (np.arange(0, 64, 2, dtype=np.float32) / np.float32(64)))).astype(np.float32)
    c[0:32, C_INV] = inv
    c[32:64, C_INV] = inv
    return c


def _pack_layer(inputs, l):
    g = lambda n: np.asarray(inputs[n][l], np.float32)
    vec = np.zeros((128, NV), np.float32)
    col = lambda v: np.ascontiguousarray(v.reshape(-1, 128).T)
    vec[:, V_LN1:V_LN1 + 8] = col(g("ln_ffn1"))
    vec[:, V_LNM:V_LNM + 8] = col(g("ln_mix"))
    vec[:, V_LN2:V_LN2 + 8] = col(g("ln_ffn2"))
    cw = g("conv_w")
    vec[:, V_CW:V_CW + 96] = cw.reshape(4, 24, 128).transpose(2, 1, 0).reshape(128, 96)
    vec[:, V_CB:V_CB + 24] = col(g("conv_b"))
    vec[:, V_QLN:V_QLN + 4] = col(g("q_lora_norm"))
    vec[:, V_KVLN:V_KVLN + 2] = col(g("kv_lora_norm"))
    qn, kn = g("q_norm"), g("k_norm")
    for (base, v) in ((V_QN, qn), (V_KN, kn)):
        vec[:, base] = v[0:128]
        vec[0:64, base + 1] = v[128:192]
        vec[0:32, base + 2] = v[160:192]
        vec[32:64, base + 2] = v[128:160]
    vec[:, V_SSDN:V_SSDN + 16] = col(g("ssd_norm"))
    rows = np.concatenate([g("dt_bias"), g("a_log"), g("d_skip")]).reshape(1, 96).astype(np.float32)
    d = {f"vecs{l}": vec, f"rows{l}": rows}
    for nm in ["ffn1_w13", "ffn1_w2", "w_in", "w_ssd_out", "w_uq", "w_ukv", "w_mla_out", "w_o", "ffn2_w13", "ffn2_w2"]:
        d[f"{nm}{l}"] = np.ascontiguousarray(g(nm))
    return d


_NC_CACHE = {}


def kernel(**inputs):
    x = np.asarray(inputs["x"], np.float32)
    B, SEQ, _ = x.shape
    DEPTH = inputs["ln_ffn1"].shape[0]
    key = (SEQ, DEPTH)
    if key not in _NC_CACHE:
        _NC_CACHE[key] = build(SEQ, DEPTH)
    nc = _NC_CACHE[key]
    shared = {"cst": _consts()}
    for l in range(DEPTH):
        shared.update(_pack_layer(inputs, l))
    in_maps = []
    for b in range(B):
        m = dict(shared)
        m["xT"] = np.ascontiguousarray(x[b].T)
        m["pos"] = np.ascontiguousarray(np.asarray(inputs["positions"][b], np.int32).reshape(1, SEQ))
        in_maps.append(m)
    res = run_bass_kernel_spmd(nc, in_maps, core_ids=list(range(B)))
    out = np.stack([np.ascontiguousarray(res.results[b]["oT"].T) for b in range(B)], axis=0)
    return out.astype(np.float32)
```

```python
import math
import time
from contextlib import ExitStack

import numpy as np
import concourse.bass as bass
import concourse.mybir as mybir
from concourse.bass_utils import run_bass_kernel_spmd

dt = mybir.dt
F32, BF16, I32 = dt.float32, dt.bfloat16, dt.int32
AF = mybir.ActivationFunctionType
ALU = mybir.AluOpType

SAME_ENG_SYNC = True
import os as _os
PHASES = _os.environ.get("KPH", "f1,nm,mla,ssd,f2").split(",")
KMLA = int(_os.environ.get("KMLA", "99"))
KSSD = int(_os.environ.get("KSSD", "999999"))
KV = int(_os.environ.get("KV", "99"))


class Tk:
    __slots__ = ("name", "lw", "rd", "sem", "dcnt")

    def __init__(self, name):
        self.name = name
        self.lw = None
        self.rd = []
        self.sem = None
        self.dcnt = 0


class Op:
    __slots__ = ("eng", "fn", "R", "W", "dma", "deps", "sig", "sem", "cnt", "name", "grp", "semtok")


class Prog:
    ENGS = ["pe", "act", "dve", "pool", "sp"]

    def __init__(self, nc, stack):
        self.nc = nc
        self.stack = stack
        self.ops = []
        self.inserts = {}
        self.nsem = 0

    def sbuf(self, name, shape, dtype):
        return self.stack.enter_context(self.nc.sbuf_tensor(name, list(shape), dtype))

    def psum(self, name, shape, dtype):
        return self.stack.enter_context(self.nc.psum_tensor(name, list(shape), dtype))

    def sem(self, name):
        self.nsem += 1
        return self.stack.enter_context(self.nc.semaphore(name))

    def mkop(self, eng, fn, R=(), W=(), dma=False, name="", semtok=None, grp=None):
        op = Op()
        op.eng, op.fn, op.R, op.W, op.dma, op.name = eng, fn, tuple(R), tuple(W), dma, name
        op.sig = False
        op.sem = None
        op.cnt = 0
        op.grp = grp
        op.semtok = semtok
        return op

    limit = None

    def add(self, *a, **k):
        op = self.mkop(*a, **k)
        if self.limit is not None and len(self.ops) >= self.limit:
            return op
        self.ops.append(op)
        return op

    def insert_before(self, pos, op):
        self.inserts.setdefault(pos, []).append(op)

    def finalize(self):
        final = []
        for i, op in enumerate(self.ops):
            if i in self.inserts:
                final.extend(self.inserts[i])
            final.append(op)
        if len(self.ops) in self.inserts:
            final.extend(self.inserts[len(self.ops)])
        self.final = final
        for op in final:
            deps = {}
            for t in op.R:
                if t.lw is not None:
                    deps[id(t.lw)] = t.lw
            for t in op.W:
                if t.lw is not None:
                    deps[id(t.lw)] = t.lw
                for x in t.rd:
                    deps[id(x)] = x
            deps.pop(id(op), None)
            op.deps = [d for d in deps.values() if not (op.grp is not None and d.grp == op.grp)]
            for t in op.R:
                if op.dma:
                    t.rd.append(op)
                else:
                    t.rd = [x for x in t.rd if x.dma or x.eng != op.eng]
                    t.rd.append(op)
            for t in op.W:
                t.lw = op
                t.rd = []
        for op in final:
            if op.dma:
                tok = op.semtok if op.semtok is not None else (op.W[0] if op.W else op.R[0])
                if tok.sem is None:
                    tok.sem = self.sem("d_" + tok.name)
                tok.dcnt += 16
                op.sem = tok.sem
                op.cnt = tok.dcnt
        gmax = {}
        for op in final:
            if op.dma and op.grp is not None:
                k = (op.grp, id(op.sem))
                gmax[k] = max(gmax.get(k, 0), op.cnt)
        for op in final:
            if op.dma and op.grp is not None:
                op.cnt = gmax[(op.grp, id(op.sem))]
        for op in final:
            for d in op.deps:
                if d.dma:
                    continue
                if d.eng != op.eng:
                    d.sig = True
                elif SAME_ENG_SYNC and op.eng != "pe" and not op.dma:
                    d.sig = True
        self.esem = {e: self.sem("e_" + e) for e in self.ENGS}
        ecnt = {e: 0 for e in self.ENGS}
        for op in final:
            if not op.dma and op.sig:
                ecnt[op.eng] += 1
                op.sem = self.esem[op.eng]
                op.cnt = ecnt[op.eng]
        self.by_eng = {e: [op for op in final if op.eng == e] for e in self.ENGS}

    def _run(self, ename, e):
        known = {}
        for op in self.by_eng[ename]:
            waits = {}
            for d in op.deps:
                if not d.dma:
                    if d.eng == ename and (ename == "pe" or not SAME_ENG_SYNC or op.dma):
                        continue
                    if not d.sig:
                        continue
                k = id(d.sem)
                if k not in waits or waits[k][1] < d.cnt:
                    waits[k] = (d.sem, d.cnt)
            for k, (s, v) in waits.items():
                if known.get(k, 0) >= v:
                    continue
                known[k] = v
                e.wait_ge(s, v)
            if op.fn is None:
                continue
            ins = op.fn(e)
            if op.dma:
                ins.then_inc(op.sem, 16)
            elif op.sig:
                ins.then_inc(op.sem, 1)

    def emit(self):
        with self.nc.Block() as blk:
            @blk.tensor
            def _(e):
                self._run("pe", e)

            @blk.scalar
            def _(e):
                self._run("act", e)

            @blk.vector
            def _(e):
                self._run("dve", e)

            @blk.gpsimd
            def _(e):
                self._run("pool", e)

            @blk.sync
            def _(e):
                self._run("sp", e)


class WStream:
    def __init__(self, P, name, shape, dtype, nslots, la, eng="sp"):
        self.P = P
        self.name = name
        self.n = nslots
        self.la = la
        self.eng = eng
        self.buf = [P.sbuf(f"{name}{i}", shape, dtype) for i in range(nslots)]
        self.tok = [Tk(f"{name}{i}") for i in range(nslots)]
        self.reqs = []
        self.minpos = []

    def get(self, pieces, extraR=(), minpos=0):
        if self.P.limit is not None and len(self.P.ops) >= self.P.limit:
            return self.buf[0], self.tok[0]
        n = len(self.reqs)
        self.minpos.append(minpos)
        self.reqs.append((len(self.P.ops), pieces, tuple(extraR)))
        s = n % self.n
        return self.buf[s], self.tok[s]

    def flush(self):
        P = self.P
        import bisect
        tid = {id(t): i for i, t in enumerate(self.tok)}
        reads = [[] for _ in self.tok]
        for pos, op in enumerate(P.ops):
            for t in op.R:
                i = tid.get(id(t))
                if i is not None:
                    reads[i].append(pos)
        nreq = len(self.reqs)
        last_use = [0] * nreq
        for n in range(nreq):
            s = n % self.n
            lo = self.reqs[n][0]
            hi = self.reqs[n + self.n][0] if n + self.n < nreq else len(P.ops) + 1
            r = reads[s]
            a = bisect.bisect_left(r, lo)
            b = bisect.bisect_left(r, hi)
            last_use[n] = r[b - 1] if b > a else lo
        for n, (pos, pieces, extraR) in enumerate(self.reqs):
            s = n % self.n
            ipos = self.reqs[max(0, n - self.la)][0]
            if n - self.n >= 0:
                ipos = max(ipos, last_use[n - self.n] + 1)
            ipos = max(ipos, self.minpos[n])
            assert ipos <= pos, (self.name, n, ipos, pos)
            for (dstf, src) in pieces:
                dst = dstf(self.buf[s])
                op = P.mkop(self.eng, (lambda e, d=dst, s_=src: e.dma_start(out=d, in_=s_)),
                            R=extraR, W=[self.tok[s]], dma=True, name=f"ld_{self.name}{n}",
                            semtok=self.tok[s], grp=(self.name, n))
                P.insert_before(ipos, op)


class V:
    __slots__ = ("ap", "tk")

    def __init__(self, ap, tk):
        self.ap = ap
        self.tk = list(tk)


class Arena:
    GR = 1024

    def __init__(self, P, name, nbytes):
        self.nbytes = nbytes
        self.t = P.sbuf(name, [128, nbytes // 4], F32)
        self.tb = self.t.bitcast(BF16)
        self.ti = self.t.bitcast(I32)
        self.g = [Tk(f"{name}_g{i}") for i in range((nbytes + self.GR - 1) // self.GR)]

    def toks(self, lo, hi):
        return self.g[lo // self.GR:(hi + self.GR - 1) // self.GR]


class Buf:
    def __init__(self, P, name, C, T, dtype, arena=None, off=None):
        esz = 2 if dtype == BF16 else 4
        self.C, self.T, self.cb = C, T, T * esz
        self.arena = arena
        self.off = off
        if arena is None:
            self.t = P.sbuf(name, [128, C, T], dtype)
            self.ap = self.t[:]
            self.own = [Tk(f"{name}_{c}") for c in range(C)]
        else:
            assert off % 4 == 0 and off + C * T * esz <= arena.nbytes, (name, off, C, T)
            base = arena.tb if dtype == BF16 else (arena.ti if dtype == I32 else arena.t)
            e0 = off // esz
            self.ap = base[:, e0:e0 + C * T].rearrange("p (c t) -> p c t", t=T)

    def toks(self, c0, c1=None):
        c1 = c0 + 1 if c1 is None else c1
        if self.arena is None:
            return self.own[c0:c1]
        return self.arena.toks(self.off + c0 * self.cb, self.off + c1 * self.cb)

    def c(self, c, lo=None, hi=None, p0=0, p1=128):
        return V(self.ap[p0:p1, c, lo:hi], self.toks(c))

    def r(self, c0, c1, p0=0, p1=128):
        return V(self.ap[p0:p1, c0:c1, :], self.toks(c0, c1))

    def all(self):
        return self.r(0, self.C)


D = 1024
DFF = 2816
T = 512
NH = 32
HD = 64
NG = 4
NS = 128
DIN = 2048
MH = 8
QL = 512
KVL = 256
EPS = 1e-6
WIN_B = 8096
V_LN1, V_LNM, V_LN2, V_CW, V_CB, V_QLN, V_KVLN = 0, 8, 16, 24, 120, 144, 148
V_QN, V_QR, V_QRS, V_KN, V_KR, V_KRS, V_SSDN, NV = 150, 151, 152, 153, 154, 155, 156, 172
C_ID, C_U, C_INV, NCST = 0, 128, 256, 257
PI = 3.1415925
TWO_PI = 6.283185307179586
CW1 = 6.28125
CW2 = TWO_PI - 6.28125


def build(SEQ, DEPTH, dbg=None):
    NT = SEQ // T
    nc = bass.Bass("TRN2", target_bir_lowering=False)
    inp = lambda n, s, d=F32: nc.dram_tensor(n, list(s), d, kind="ExternalInput").ap()
    scr = lambda n, s, d=BF16: nc.dram_tensor(n, list(s), d, kind="Internal").ap()
    xT = inp("xT", [D, SEQ])
    pos = inp("pos", [1, SEQ], I32)
    cst = inp("cst", [128, NCST])
    oT = nc.dram_tensor("oT", [D, SEQ], F32, kind="ExternalOutput").ap()
    W = []
    for l in range(DEPTH):
        w = {}
        w["vecs"] = inp(f"vecs{l}", [128, NV])
        w["rows"] = inp(f"rows{l}", [1, 96])
        for nm, sh in [("ffn1_w13", [D, 2 * DFF]), ("ffn1_w2", [DFF, D]), ("w_in", [D, 8032]),
                       ("w_ssd_out", [DIN, D]), ("w_uq", [QL, 1536]), ("w_ukv", [KVL, 2048]),
                       ("w_mla_out", [D, D]), ("w_o", [D, D]), ("ffn2_w13", [D, 2 * DFF]), ("ffn2_w2", [DFF, D])]:
            w[nm] = inp(f"{nm}{l}", sh)
        for nm, sh in [("ffn1_w13", [D, 2 * DFF]), ("ffn1_w2", [DFF, D]), ("w_in", [D, WIN_B]),
                       ("w_ssd_out", [DIN, D]), ("w_uq", [QL, 2048]), ("w_ukv", [KVL, 2048]),
                       ("w_mla_out", [D, D]), ("w_o", [D, D]), ("ffn2_w13", [D, 2 * DFF]), ("ffn2_w2", [DFF, D])]:
            w[nm + "_b"] = scr(f"{nm}{l}_b", sh)
            w[nm + "_tk"] = Tk(f"{nm}{l}_b")
        W.append(w)
    hbuf = scr("hbuf", [D, SEQ], F32)
    KN = scr("KN", [MH, NT, 128, T])
    KR = scr("KR", [NT, 64, T])
    VV = scr("VV", [MH, NT, 128, 4, 128])
    dbg_out = None
    if dbg:
        dbg_out = nc.dram_tensor("dbg", [128, dbg], F32, kind="ExternalOutput").ap()

    st = ExitStack()
    with st:
        P = Prog(nc, st)

        def MM(o, l, r, start=True, stop=True, extraR=()):
            P.add("pe", lambda e: e.matmul(o.ap, lhsT=l.ap, rhs=r.ap, start=start, stop=stop),
                  R=l.tk + r.tk + list(extraR), W=o.tk, name="mm")

        def TR(o, i, ident):
            P.add("pe", lambda e: e.transpose(o.ap, i.ap, ident.ap), R=i.tk + ident.tk, W=o.tk)

        def ACT(o, i, func, bias=None, scale=None, accum=None, extraR=()):
            kw = {}
            R = i.tk + list(extraR)
            Wt = list(o.tk)
            if bias is not None:
                if isinstance(bias, V):
                    kw["bias"] = bias.ap
                    R += bias.tk
                else:
                    kw["bias"] = bias
            if scale is not None:
                if isinstance(scale, V):
                    kw["scale"] = scale.ap
                    R += scale.tk
                else:
                    kw["scale"] = scale
            if accum is not None:
                kw["accum_out"] = accum.ap
                Wt += accum.tk
            P.add("act", lambda e: e.activation(out=o.ap, in_=i.ap, func=func, **kw), R=R, W=Wt, name="act_" + str(func).split(".")[-1])

        def TT(eng, o, a, b, op):
            P.add(eng, lambda e: e.tensor_tensor(out=o.ap, in0=a.ap, in1=b.ap, op=op), R=a.tk + b.tk, W=o.tk)

        def TS(eng, o, a, s1, op0, s2=None, op1=None):
            R = list(a.tk)
            k1 = s1
            if isinstance(s1, V):
                R += s1.tk
                k1 = s1.ap
            k2 = s2
            if isinstance(s2, V):
                R += s2.tk
                k2 = s2.ap
            if op1 is None:
                P.add(eng, lambda e: e.tensor_scalar(out=o.ap, in0=a.ap, scalar1=k1, scalar2=None, op0=op0), R=R, W=o.tk)
            else:
                P.add(eng, lambda e: e.tensor_scalar(out=o.ap, in0=a.ap, scalar1=k1, scalar2=k2, op0=op0, op1=op1), R=R, W=o.tk)

        def STT(o, a, s, b, op0, op1):
            R = a.tk + b.tk
            k = s
            if isinstance(s, V):
                R += s.tk
                k = s.ap
            P.add("dve", lambda e: e.scalar_tensor_tensor(out=o.ap, in0=a.ap, scalar=k, in1=b.ap, op0=op0, op1=op1), R=R, W=o.tk)

        def CP(eng, o, i):
            if eng == "act":
                P.add("act", lambda e: e.activation(out=o.ap, in_=i.ap, func=AF.Copy), R=i.tk, W=o.tk, name="act_cp")
            else:
                P.add(eng, lambda e: e.tensor_copy(out=o.ap, in_=i.ap), R=i.tk, W=o.tk)

        def RCP(o, i):
            P.add("dve", lambda e: e.reciprocal(out=o.ap, in_=i.ap), R=i.tk, W=o.tk)

        def MEMSET(eng, o, val):
            P.add(eng, lambda e: e.memset(o.ap, val), W=o.tk)

        def DMA(eng, out_ap, in_ap, R=(), W=(), semtok=None, grp=None):
            return P.add(eng, lambda e: e.dma_start(out=out_ap, in_=in_ap), R=R, W=W, dma=True, semtok=semtok, grp=grp)

        def conv_w(src, dst, tk, pieces, rows, grp):
            for r0 in range(0, rows, 128):
                for (dc, sc, n) in pieces:
                    DMA("pool", dst[r0:r0 + 128, dc:dc + n], src[r0:r0 + 128, sc:sc + n], W=[tk], semtok=tk, grp=grp)

        win_pieces = [(0, 0, 5120), (5120, 5152, 512), (5632, 5664, 256), (5888, 5920, 64), (5952, 5952, 32),
                      (5984, 5920, 32), (6016, 5120, 32), (6048, 5984, 2048)]
        uq_pieces = []
        for h in range(MH):
            uq_pieces += [(128 * h, 192 * h, 128), (1024 + 64 * h, 192 * h + 128, 64),
                          (1536 + 64 * h, 192 * h + 160, 32), (1536 + 64 * h + 32, 192 * h + 128, 32)]
        ukv_pieces = []
        for h in range(MH):
            ukv_pieces += [(128 * h, 256 * h, 128), (1024 + 128 * h, 256 * h + 128, 128)]
        for l in range(DEPTH):
            w = W[l]
            order = [("ffn1_w13", [(0, 0, 2 * DFF)], D), ("ffn1_w2", [(0, 0, D)], DFF), ("w_in", win_pieces, D),
                     ("w_uq", uq_pieces, QL), ("w_ukv", ukv_pieces, KVL), ("w_mla_out", [(0, 0, D)], D),
                     ("w_ssd_out", [(0, 0, D)], DIN), ("w_o", [(0, 0, D)], D),
                     ("ffn2_w13", [(0, 0, 2 * DFF)], D), ("ffn2_w2", [(0, 0, D)], DFF)]
            for nm, pcs, rows in order:
                conv_w(w[nm], w[nm + "_b"], w[nm + "_tk"], pcs, rows, grp=f"cv_{nm}{l}")

        cstb = Buf(P, "cstb", 1, NCST, F32)
        DMA("sp", cstb.ap[:, 0, :], cst, W=cstb.toks(0), semtok=cstb.toks(0)[0])
        identb = Buf(P, "identb", 1, 128, BF16)
        Ub = Buf(P, "Ub", 1, 128, BF16)
        onesb = Buf(P, "onesb", 1, 128, BF16)
        onesf = Buf(P, "onesf", 1, 128, F32)
        ident_f = V(cstb.ap[:, 0, C_ID:C_ID + 128], cstb.toks(0))
        U_f = V(cstb.ap[:, 0, C_U:C_U + 128], cstb.toks(0))
        invf = V(cstb.ap[0:64, 0, C_INV:C_INV + 1], cstb.toks(0))
        CP("dve", identb.c(0), ident_f)
        CP("dve", Ub.c(0), U_f)
        MEMSET("dve", onesb.c(0), 1.0)
        MEMSET("dve", onesf.c(0), 1.0)
        ident_b = identb.c(0)
        epsb = Buf(P, "epsb", 1, 2, F32)
        MEMSET("dve", epsb.c(0), EPS)
        epsc = V(epsb.ap[:, 0, 0:1], epsb.toks(0))
        negm = Buf(P, "negm", 1, 512, BF16)
        for q_ in range(4):
            TS("dve", negm.c(0, q_ * 128, (q_ + 1) * 128), U_f, -1.0, ALU.add, 30000.0, ALU.mult)
        U_b = Ub.c(0)
        ones_b = onesb.c(0)
        ones_f = onesf.c(0)

        hT = Buf(P, "hT", 8, T, F32)
        xn = Buf(P, "xn", 8, T, BF16)
        mg = Buf(P, "mg", 8, T, BF16)
        Sst = Buf(P, "Sst", NG, 512, F32)
        Sb = Buf(P, "Sb", NG, 512, BF16)
        sck = Buf(P, "sck", NT * 4, 8, F32)
        halo = Buf(P, "halo", 24, 3, F32)
        vec = [Buf(P, f"vec{l}", 1, NV, F32) for l in range(DEPTH)]
        rowsb = [Buf(P, f"rowsb{l}", 1, 96, F32) for l in range(DEPTH)]
        aneg = [Buf(P, f"aneg{l}", 1, 32, F32) for l in range(DEPTH)]
        for l in range(DEPTH):
            DMA("sp", vec[l].ap[:, 0, :], W[l]["vecs"], W=vec[l].toks(0), semtok=vec[l].toks(0)[0])
            DMA("sp", rowsb[l].ap[:, 0, :], W[l]["rows"][0, :].partition_broadcast(128), W=rowsb[l].toks(0), semtok=rowsb[l].toks(0)[0])
            ACT(aneg[l].c(0), rowsb[l].c(0, 32, 64), AF.Exp)
            TS("dve", aneg[l].c(0), aneg[l].c(0), -1.0, ALU.mult)
        ws = WStream(P, "ws", [128, 8, 512], BF16, 4, 2)
        kvs = WStream(P, "kvs", [128, 1536], BF16, 4, 3)
        for i_ in range(4):
            MEMSET("pool", V(kvs.buf[i_][64:128, 512:1024], [kvs.tok[i_]]), 0.0)
        ps = [P.psum(f"ps{i}", [128, 512], F32) for i in range(8)]
        psb = [p.bitcast(BF16) for p in ps]
        tk_ps = [Tk(f"ps{i}") for i in range(8)]

        def PS(i, lo=0, hi=512, p0=0, p1=128):
            return V(ps[i][p0:p1, lo:hi], [tk_ps[i]])

        def PSB(i, lo=0, hi=1024, p0=0, p1=128):
            return V(psb[i][p0:p1, lo:hi], [tk_ps[i]])

        AR = Arena(P, "arena", 110 * 1024)
        KB = 1024
        sq = Buf(P, "sq", 2, T, BF16, AR, 0)
        rs = Buf(P, "rs", 1, T, F32, AR, 2 * KB)
        rstd = Buf(P, "rstd", 1, T, F32, AR, 4 * KB)
        sg = Buf(P, "sg", 2, T, F32, AR, 6 * KB)
        B0 = 10 * KB
        act = Buf(P, "act", 22, T, BF16, AR, B0)
        o = B0
        cqn = Buf(P, "cqn", 4, T, BF16, AR, o); o += 4 * KB
        ckvn = Buf(P, "ckvn", 2, T, BF16, AR, o); o += 2 * KB
        Qn = Buf(P, "Qn", 8, T, BF16, AR, o); o += 8 * KB
        Qr = Buf(P, "Qr", 8, T, BF16, AR, o); o += 8 * KB
        KTs = Buf(P, "KTs", 8, T, BF16, AR, o); o += 8 * KB
        KrT = Buf(P, "KrT", 1, T, BF16, AR, o); o += 1 * KB
        Vs = Buf(P, "Vs", 4, 1024, BF16, AR, o); o += 8 * KB
        PT = Buf(P, "PT", 4, T, BF16, AR, o); o += 4 * KB
        oTb = Buf(P, "oTb", 8, T, BF16, AR, o); o += 8 * KB
        cos2 = Buf(P, "cos2", 1, T, F32, AR, o); o += 2 * KB
        sinS = Buf(P, "sinS", 1, T, F32, AR, o); o += 2 * KB
        ra = Buf(P, "ra", 1, T, F32, AR, o); o += 2 * KB
        rb = Buf(P, "rb", 1, T, F32, AR, o); o += 2 * KB
        rstdq = Buf(P, "rstdq", 1, T, F32, AR, o); o += 2 * KB
        rd = Buf(P, "rd", 2, T, F32, AR, o); o += 4 * KB
        posi = Buf(P, "posi", 1, T, I32, AR, o); o += 2 * KB
        ksq = Buf(P, "ksq", 1, 128, F32, AR, o); o += 1 * KB
        mla_end = o
        o = B0
        zs = Buf(P, "zs", 4, DIN, BF16, AR, o); o += 16 * KB
        dAU = Buf(P, "dAU", 2, 1024, F32, AR, o)
        stg = Buf(P, "stg", 2, 516, F32, AR, o); o += 5 * KB
        cacc = Buf(P, "cacc", 2, T, F32, AR, o); o += 4 * KB
        xsT = Buf(P, "xsT", 16, T, BF16, AR, o); o += 16 * KB
        BT = Buf(P, "BT", 4, T, BF16, AR, o); o += 4 * KB
        CT = Buf(P, "CT", 4, T, BF16, AR, o); o += 4 * KB
        xdt = Buf(P, "xdt", 1, DIN, BF16, AR, o); o += 4 * KB
        xw = Buf(P, "xw", 1, DIN, BF16, AR, o); o += 4 * KB
        xtok = Buf(P, "xtok", 1, DIN, BF16, AR, o); o += 4 * KB
        Btok = Buf(P, "Btok", 1, 512, BF16, AR, o); o += 1 * KB
        Eb = Buf(P, "Eb", 2, 1024, BF16, AR, o); o += 4 * KB
        Mt = Buf(P, "Mt", 2, 1024, BF16, AR, o); o += 4 * KB
        CBm = Buf(P, "CBm", 1, 512, F32, AR, o); o += 2 * KB
        yb = Buf(P, "yb", 1, DIN, F32, AR, o); o += 8 * KB
        ynb = Buf(P, "ynb", 1, DIN, BF16, AR, o); o += 4 * KB
        tmpa = Buf(P, "tmpa", 2, 512, F32, AR, o); o += 4 * KB
        tmpb = Buf(P, "tmpb", 2, 512, F32, AR, o); o += 4 * KB
        dts = Buf(P, "dts", 8, 128, F32, AR, o); o += 4 * KB
        ksq2 = Buf(P, "ksq2", 1, 8, F32, AR, o); o += 1 * KB
        ssd_end = o
        assert mla_end <= AR.nbytes and ssd_end <= AR.nbytes, (mla_end, ssd_end)
        ynT = xsT

        xTv = xT.rearrange("(c p) t -> p c t", p=128)
        oTv = oT.rearrange("(c p) t -> p c t", p=128)
        hbv = hbuf.rearrange("(c p) t -> p c t", p=128)
        tk_hb = [Tk(f"hb{i}") for i in range(NT)]
        tk_kvc = [Tk(f"kvc{i}") for i in range(NT)]
        kv_store_pos = {}
        out_ops = []

        def wslab(w, nm, kc0, kc1, c0, ncols, dcol=0):
            src = w[nm + "_b"].rearrange("(c p) n -> p c n", p=128)[:, kc0:kc1, c0:c0 + ncols]
            n = kc1 - kc0
            return (lambda b: b[:, 0:n, dcol:dcol + ncols], src)

        def rmsnorm_to_xn(gcol, l):
            for c in range(8):
                ACT(sq.c(c % 2), hT.c(c), AF.Square)
                MM(PS(7), ones_b, sq.c(c % 2), start=(c == 0), stop=(c == 7))
            ACT(rs.c(0), PS(7), AF.Sqrt, bias=EPS, scale=1.0 / D)
            RCP(rstd.c(0), rs.c(0))
            for c in range(8):
                STT(xn.c(c), hT.c(c), V(vec[l].ap[:, 0, gcol + c:gcol + c + 1], vec[l].toks(0)), rstd.c(0), ALU.mult, ALU.mult)

        def ffn(l, which):
            w = W[l]
            n13, n2 = f"ffn{which}_w13", f"ffn{which}_w2"
            rmsnorm_to_xn(V_LN1 if which == 1 else V_LN2, l)
            for j in range(11):
                c0 = j * 256
                buf, tk = ws.get([wslab(w, n13, 0, 8, c0, 256, 0), wslab(w, n13, 0, 8, DFF + c0, 256, 256)], extraR=[w[n13 + "_tk"]])
                pb = 0 if j % 2 == 0 else 4
                for q in range(4):
                    for k in range(8):
                        MM(PS(pb + q), V(buf[:, k, q * 128:(q + 1) * 128], [tk]), xn.c(k), start=(k == 0), stop=(k == 7))
                for q in range(2):
                    b = (2 * j + q) % 2
                    ACT(sg.c(b), PS(pb + q), AF.Silu)
                    TT("dve", act.c(2 * j + q), sg.c(b), PS(pb + q + 2), ALU.mult)
            for half in range(2):
                pb = 0 if half == 0 else 4
                for (k0, k1) in [(0, 8), (8, 16), (16, 22)]:
                    buf, tk = ws.get([wslab(w, n2, k0, k1, half * 512, 512)], extraR=[w[n2 + "_tk"]])
                    for k in range(k0, k1):
                        for q in range(4):
                            MM(PS(pb + q), V(buf[:, k - k0, q * 128:(q + 1) * 128], [tk]), act.c(k), start=(k == 0), stop=(k == 21))
                for q in range(4):
                    c = half * 4 + q
                    STT(hT.c(c), PS(pb + q), 0.5, hT.c(c), ALU.mult, ALU.add)

        def vcol(l, col, p0=0, p1=128):
            return V(vec[l].ap[p0:p1, 0, col:col + 1], vec[l].toks(0))

        def rope_tables(t0):
            DMA("sp", posi.ap[0:64, 0, :], pos[0, t0:t0 + T].partition_broadcast(64), W=posi.toks(0), semtok=posi.toks(0)[0])
            ang = V(ra.ap[0:64, 0, :], ra.toks(0))
            kk = V(rb.ap[0:64, 0, :], rb.toks(0))
            CP("dve", ang, V(posi.ap[0:64, 0, :], posi.toks(0)))
            TS("dve", ang, ang, invf, ALU.mult)
            for (dst, shift) in ((sinS, 0.0), (cos2, math.pi / 2)):
                d = V(dst.ap[0:64, 0, :], dst.toks(0))
                TS("dve", kk, ang, 1.0 / TWO_PI, ALU.mult, shift / TWO_PI, ALU.add)
                TS("dve", kk, kk, 12582912.0, ALU.add)
                TS("dve", kk, kk, -12582912.0, ALU.add)
                STT(d, kk, -CW1, ang, ALU.mult, ALU.add)
                STT(d, kk, -CW2, d, ALU.mult, ALU.add)
                TS("dve", d, d, shift, ALU.add, -PI, ALU.max)
                TS("dve", d, d, PI, ALU.min)
                ACT(d, d, AF.Sin)
            TS("dve", V(sinS.ap[0:32, 0, :], sinS.toks(0)), V(sinS.ap[0:32, 0, :], sinS.toks(0)), -1.0, ALU.mult)

        def mla(l, it):
            w = W[l]
            t0 = it * T
            rope_tables(t0)
            cs = V(cos2.ap[0:64, 0, :], cos2.toks(0))
            sn = V(sinS.ap[0:64, 0, :], sinS.toks(0))
            wt = [w["w_in_tk"]]
            if KMLA < 1:
                return []
            MEMSET("pool", V(KrT.ap[64:128, 0, :], KrT.toks(0)), 0.0)
            MEMSET("pool", V(Qr.ap[64:128, :, :], Qr.toks(0, 8)), 0.0)
            buf, tk = ws.get([wslab(w, "w_in", 0, 8, 5120, 512)], extraR=wt)
            for q in range(4):
                for k in range(8):
                    MM(PS(q), V(buf[:, k, q * 128:(q + 1) * 128], [tk]), xn.c(k), start=(k == 0), stop=(k == 7))
            for q in range(4):
                if KV >= 2:
                    ACT(sq.c(q % 2), PS(q), AF.Square)
                    MM(PS(7), ones_b, sq.c(q % 2), start=(q == 0), stop=(q == 3))
                if KV >= 3:
                    ACT(cqn.c(q), PS(q), AF.Copy, scale=vcol(l, V_QLN + q))
            if KV >= 2:
                ACT(rs.c(0), PS(7), AF.Sqrt, bias=EPS, scale=1.0 / QL)
                RCP(rstd.c(0), rs.c(0))
            if KV >= 4:
                for q in range(4):
                    TT("dve", cqn.c(q), cqn.c(q), rstd.c(0), ALU.mult)
            if KMLA < 2:
                return []
            buf, tk = ws.get([wslab(w, "w_in", 0, 8, 5632, 384)], extraR=wt)
            for q in range(2):
                for k in range(8):
                    MM(PS(4 + q), V(buf[:, k, q * 128:(q + 1) * 128], [tk]), xn.c(k), start=(k == 0), stop=(k == 7))
            for k in range(8):
                MM(PS(6, 0, 512, 0, 64), V(buf[:, k, 256:320], [tk]), xn.c(k), start=(k == 0), stop=(k == 7))
            for k in range(8):
                MM(PS(3, 0, 512, 0, 64), V(buf[:, k, 320:384], [tk]), xn.c(k), start=(k == 0), stop=(k == 7))
            for q in range(2):
                ACT(sq.c(q % 2), PS(4 + q), AF.Square)
                MM(PS(7), ones_b, sq.c(q % 2), start=(q == 0), stop=(q == 1))
                ACT(ckvn.c(q), PS(4 + q), AF.Copy, scale=vcol(l, V_KVLN + q))
            ACT(rs.c(0), PS(7), AF.Sqrt, bias=EPS, scale=1.0 / KVL)
            RCP(rstd.c(0), rs.c(0))
            for q in range(2):
                TT("dve", ckvn.c(q), ckvn.c(q), rstd.c(0), ALU.mult)
            a64 = V(ra.ap[0:64, 0, :], ra.toks(0))
            b64 = V(rb.ap[0:64, 0, :], rb.toks(0))
            ACT(V(sq.ap[0:64, 0, :], sq.toks(0)), PS(6, 0, 512, 0, 64), AF.Square)
            ACT(a64, PS(6, 0, 512, 0, 64), AF.Copy, scale=vcol(l, V_KR, 0, 64))
            TT("dve", a64, a64, cs, ALU.mult)
            ACT(b64, PS(3, 0, 512, 0, 64), AF.Copy, scale=vcol(l, V_KRS, 0, 64))
            TT("dve", b64, b64, sn, ALU.mult)
            TT("dve", V(KrT.ap[0:64, 0, :], KrT.toks(0)), a64, b64, ALU.add)
            for cc in range(4):
                MM(PS(7, cc * 9 + 8, cc * 9 + 9), V(sq.ap[0:64, 0, cc * 128:(cc + 1) * 128], sq.toks(0)),
                   V(onesb.ap[0:64, 0, 0:1], onesb.toks(0)))
            if KMLA < 3:
                return []
            wt2 = [w["w_ukv_tk"]]
            for half in range(2):
                buf, tk = ws.get([wslab(w, "w_ukv", 0, 2, half * 512, 512)], extraR=wt2)
                for q in range(4):
                    h = half * 4 + q
                    pb = q % 2
                    for k in range(2):
                        MM(PS(pb), V(buf[:, k, q * 128:(q + 1) * 128], [tk]), ckvn.c(k), start=(k == 0), stop=(k == 1))
                    ACT(sq.c(pb), PS(pb), AF.Square)
                    for cc in range(4):
                        MM(PS(7, cc * 9 + h, cc * 9 + h + 1), sq.c(pb, cc * 128, (cc + 1) * 128), V(onesb.ap[:, 0, 0:1], onesb.toks(0)))
                    ACT(KTs.c(h), PS(pb), AF.Copy, scale=vcol(l, V_KN))
            CP("act", V(ksq.ap[:, 0, 64:100], ksq.toks(0)), PS(7, 0, 36))
            for cc in range(4):
                kq = V(ksq.ap[:, 0, cc * 8:(cc + 1) * 8], ksq.toks(0))
                TS("dve", kq, V(ksq.ap[:, 0, 64 + cc * 9:64 + cc * 9 + 8], ksq.toks(0)),
                   V(ksq.ap[:, 0, 64 + cc * 9 + 8:64 + cc * 9 + 9], ksq.toks(0)), ALU.add)
                ACT(kq, kq, AF.Sqrt, bias=EPS * 192.0, scale=1.0)
                RCP(sck.c(it * 4 + cc), kq)
            if KMLA < 4:
                return []
            for half in range(2):
                buf, tk = ws.get([wslab(w, "w_ukv", 0, 2, 1024 + half * 512, 512)], extraR=wt2)
                for cc in range(4):
                    pb = 2 + cc
                    for k in range(2):
                        MM(PS(pb), ckvn.c(k, cc * 128, (cc + 1) * 128), V(buf[:, k, :], [tk]), start=(k == 0), stop=(k == 1))
                    CP("act", Vs.c(cc, half * 512, (half + 1) * 512), PS(pb))
            if KMLA < 5:
                return []
            st_ops = []
            for h in range(MH):
                st_ops.append(DMA("pool", KN[h, it], KTs.ap[:, h, :], R=KTs.toks(h), W=[tk_kvc[it]], semtok=tk_kvc[it], grp=("kvst", l, it)))
                st_ops.append(DMA("pool", VV[h, it], Vs.ap[:, :, h * 128:(h + 1) * 128], R=Vs.toks(0, 4), W=[tk_kvc[it]], semtok=tk_kvc[it], grp=("kvst", l, it)))
            st_ops.append(DMA("pool", KR[it], KrT.ap[0:64, 0, :], R=KrT.toks(0), W=[tk_kvc[it]], semtok=tk_kvc[it], grp=("kvst", l, it)))
            kv_store_pos[it] = len(P.ops)
            if KMLA < 6:
                return []
            wq = [w["w_uq_tk"]]
            slabs = {}
            for h in range(MH):
                if h % 4 == 0:
                    bn, tn = ws.get([wslab(w, "w_uq", 0, 4, (h // 4) * 512, 512)], extraR=wq)
                if h % 8 == 0:
                    br, tr = ws.get([wslab(w, "w_uq", 0, 4, 1024, 512)], extraR=wq)
                    bs, ts_ = ws.get([wslab(w, "w_uq", 0, 4, 1536, 512)], extraR=wq)
                hq = h % 4
                for k in range(4):
                    MM(PS(0), V(bn[:, k, hq * 128:(hq + 1) * 128], [tn]), cqn.c(k), start=(k == 0), stop=(k == 3))
                for k in range(4):
                    MM(PS(1, 0, 512, 0, 64), V(br[:, k, h * 64:(h + 1) * 64], [tr]), cqn.c(k), start=(k == 0), stop=(k == 3))
                for k in range(4):
                    MM(PS(2, 0, 512, 0, 64), V(bs[:, k, h * 64:(h + 1) * 64], [ts_]), cqn.c(k), start=(k == 0), stop=(k == 3))
                ACT(sq.c(0), PS(0), AF.Square)
                ACT(V(sq.ap[0:64, 1, :], sq.toks(1)), PS(1, 0, 512, 0, 64), AF.Square)
                MM(PS(3), ones_b, sq.c(0), start=True, stop=False)
                MM(PS(3), V(onesb.ap[0:64, 0, :], onesb.toks(0)), V(sq.ap[0:64, 1, :], sq.toks(1)), start=False, stop=True)
                ACT(rs.c(0), PS(3), AF.Sqrt, bias=EPS, scale=1.0 / 192.0)
                RCP(rstdq.c(0), rs.c(0))
                ACT(sg.c(0), PS(0), AF.Copy, scale=vcol(l, V_QN))
                TT("dve", Qn.c(h), sg.c(0), rstdq.c(0), ALU.mult)
                ACT(a64, PS(1, 0, 512, 0, 64), AF.Copy, scale=vcol(l, V_QR, 0, 64))
                TT("dve", a64, a64, cs, ALU.mult)
                ACT(b64, PS(2, 0, 512, 0, 64), AF.Copy, scale=vcol(l, V_QRS, 0, 64))
                TT("dve", b64, b64, sn, ALU.mult)
                TT("pool", a64, a64, b64, ALU.add)
                TT("pool", V(Qr.ap[0:64, h, :], Qr.toks(h)), a64, V(rstdq.ap[0:64, 0, :], rstdq.toks(0)), ALU.mult)
            if KMLA < 7:
                return []
            nokv = bool(_os.environ.get("KNOKV"))
            for h in range(MH):
                po, pd = (0, 1) if h % 2 == 0 else (5, 6)
                blocks = [(j, b) for j in range(it + 1) for b in range(4)]
                nb_ = len(blocks)
                ops_ = {}

                def S_(n):
                    j, b = blocks[n]
                    if j < it and not nokv and b == 0:
                        kb, kt = kvs.get([
                            (lambda b_: b_[:, 0:512], KN[h, j]),
                            (lambda b_: b_[0:64, 512:1024], KR[j]),
                            (lambda b_: b_[:, 1024:1536].rearrange("p (c d) -> p c d", d=128), VV[h, j]),
                        ], extraR=[tk_kvc[j]], minpos=kv_store_pos[j])
                        ops_[j] = (kb, kt)
                    q0 = 128 * b if j == it else 0
                    si = (2, 3, 4, 7)[n % 4]
                    pt = n % 4
                    if j < it and not nokv:
                        kb, kt = ops_[j]
                        kn_ = V(kb[:, b * 128:(b + 1) * 128], [kt])
                        kr_ = V(kb[:, 512 + b * 128:512 + (b + 1) * 128], [kt])
                    else:
                        kn_ = KTs.c(h, b * 128, (b + 1) * 128)
                        kr_ = V(KrT.ap[:, 0, b * 128:(b + 1) * 128], KrT.toks(0))
                    MM(PS(si, q0, 512), kn_, Qn.c(h, q0, 512), start=True, stop=False)
                    MM(PS(si, q0, 512), kr_, V(Qr.ap[:, h, q0:512], Qr.toks(h)), start=False, stop=True)
                    ACT(PT.c(pt, q0, 512), PS(si, q0, 512), AF.Exp, scale=V(sck.ap[:, j * 4 + b, h:h + 1], sck.toks(j * 4 + b)))
                    if j == it:
                        TT("pool", PT.c(pt, q0, q0 + 128), PT.c(pt, q0, q0 + 128), U_b, ALU.mult)

                def PV_(n):
                    j, b = blocks[n]
                    q0 = 128 * b if j == it else 0
                    pt = n % 4
                    if j < it and not nokv:
                        kb, kt = ops_[j]
                        v_ = V(kb[:, 1024 + b * 128:1024 + (b + 1) * 128], [kt])
                    else:
                        v_ = Vs.c(b, h * 128, (h + 1) * 128)
                    MM(PS(po, q0, 512), v_, PT.c(pt, q0, 512), start=(n == 0), stop=(n == nb_ - 1))
                    MM(PS(pd, q0, 512), ones_b, PT.c(pt, q0, 512), start=(n == 0), stop=(n == nb_ - 1))

                for n in range(nb_ + 3):
                    if n < nb_:
                        S_(n)
                    if n >= 3:
                        PV_(n - 3)
                RCP(rd.c(h % 2), PS(pd))
                TT("dve", oTb.c(h), PS(po), rd.c(h % 2), ALU.mult)
            for half in range(2):
                buf, tk = ws.get([wslab(w, "w_mla_out", 0, 8, half * 512, 512)], extraR=[w["w_mla_out_tk"]])
                for q in range(4):
                    for k in range(8):
                        MM(PS(q), V(buf[:, k, q * 128:(q + 1) * 128], [tk]), oTb.c(k), start=(k == 0), stop=(k == 7))
                buf, tk = ws.get([wslab(w, "w_in", 0, 8, 6048 + 1024 + half * 512, 512)], extraR=wt)
                for q in range(4):
                    for k in range(8):
                        MM(PS(4 + q), V(buf[:, k, q * 128:(q + 1) * 128], [tk]), xn.c(k), start=(k == 0), stop=(k == 7))
                for q in range(4):
                    ACT(sg.c(q % 2), PS(4 + q), AF.Sigmoid)
                    TT("dve", mg.c(half * 4 + q), sg.c(q % 2), PS(q), ALU.mult)
            return st_ops

        ssd_start = [0]

        def ssd(l, it):
            w = W[l]
            ssd_start[0] = len(P.ops)
            wt = [w["w_in_tk"]]
            if KV == 8:
                ACT(V(rs.ap[:, 0, 0:32], rs.toks(0)), V(rs.ap[:, 0, 32:64], rs.toks(0)), AF.Copy)
            if KV == 9:
                ACT(V(rs.ap[:, 0, 0:32], rs.toks(0)), V(rs.ap[:, 0, 32:64], rs.toks(0)), AF.Copy)
                ACT(V(rs.ap[:, 0, 0:32], rs.toks(0)), V(rs.ap[:, 0, 32:64], rs.toks(0)), AF.Copy)
            for nb in [int(c_) for c_ in _os.environ.get("KNB", "0123")]:
                buf, tk = ws.get([wslab(w, "w_in", 0, 8, nb * 512, 512)], extraR=wt)
                for cc in range(4):
                    pb = (nb % 2) * 4 + cc
                    for k in range(8):
                        MM(PS(pb), xn.c(k, cc * 128, (cc + 1) * 128), V(buf[:, k, :], [tk]), start=(k == 0), stop=(k == 7))
                    ACT(zs.c(cc, nb * 512, (nb + 1) * 512), PS(pb), AF.Silu)
            def cstage1(ch, buf, tk):
                q = ch % 4
                sb = ch // 4
                pb = (sb % 2) * 4 + q
                for k in range(8):
                    MM(PS(pb), V(buf[:, k, q * 128:(q + 1) * 128], [tk]), xn.c(k), start=(k == 0), stop=(k == 7))
                s_ = ch % 2
                CP("act", stg.c(s_, 3, 515), PS(pb))
                CP("pool", stg.c(s_, 0, 3), halo.c(ch))

            def cstage2(ch):
                s_ = ch % 2
                ca = cacc.c(s_)
                TS("dve", ca, stg.c(s_, 3, 515), vcol(l, V_CW + ch * 4 + 3), ALU.mult)
                STT(ca, stg.c(s_, 2, 514), vcol(l, V_CW + ch * 4 + 2), ca, ALU.mult, ALU.add)
                STT(ca, stg.c(s_, 1, 513), vcol(l, V_CW + ch * 4 + 1), ca, ALU.mult, ALU.add)
                STT(ca, stg.c(s_, 0, 512), vcol(l, V_CW + ch * 4 + 0), ca, ALU.mult, ALU.add)
                CP("pool", halo.c(ch), stg.c(s_, 512, 515))
                dst = xsT.c(ch) if ch < 16 else (BT.c(ch - 16) if ch < 20 else CT.c(ch - 20))
                ACT(dst, ca, AF.Silu, bias=vcol(l, V_CB + ch))

            cur = None
            for ch in range(24):
                if ch % 4 == 0:
                    cur = ws.get([wslab(w, "w_in", 0, 8, 2048 + (ch // 4) * 512, 512)], extraR=wt)
                cstage1(ch, cur[0], cur[1])
                if ch >= 1:
                    cstage2(ch - 1)
            cstage2(23)
            bufd, tkd = ws.get([wslab(w, "w_in", 0, 8, 6016, 32)], extraR=wt)
            rw = rowsb[l]
            dtb3 = V(rw.ap[:, 0, 0:32].unsqueeze(1).broadcast_to([128, 4, 32]), rw.toks(0))
            an3 = V(aneg[l].ap[:, 0, :].unsqueeze(1).broadcast_to([128, 4, 32]), aneg[l].toks(0))
            Dv = V(rw.ap[:, 0, 64:96], rw.toks(0))
            for cc in range(4):
                for k in range(8):
                    MM(PS(7, cc * 32, cc * 32 + 32), xn.c(k, cc * 128, (cc + 1) * 128), V(bufd[:, k, 0:32], [tkd]), start=(k == 0), stop=(k == 7))
            dtv, dA, nac, eac, wv, ela, dtw, tmp8 = [dts.c(i) for i in range(8)]
            v3 = lambda v_: V(v_.ap.rearrange("p (c h) -> p c h", h=32), v_.tk)
            TT("dve", v3(dtv), v3(PS(7, 0, 128)), dtb3, ALU.add)
            ACT(dtv, dtv, AF.Exp)
            ACT(dtv, dtv, AF.Ln, bias=1.0)
            TT("dve", v3(dA), v3(dtv), an3, ALU.mult)
            for cc in range(4):
                MM(PS(7, 128 + cc * 32, 160 + cc * 32), U_f, V(dA.ap[:, cc * 32:(cc + 1) * 32], dA.tk))
                MM(PS(7, 256 + cc * 32, 288 + cc * 32), ones_f, V(dA.ap[:, cc * 32:(cc + 1) * 32], dA.tk))
            TS("dve", nac, PS(7, 128, 256), -1.0, ALU.mult)
            ACT(eac, PS(7, 128, 256), AF.Exp)
            TT("dve", tmp8, nac, PS(7, 256, 384), ALU.add)
            ACT(wv, tmp8, AF.Exp)
            ACT(ela, PS(7, 256, 384), AF.Exp)
            TT("dve", dtw, dtv, wv, ALU.mult)
            r3 = lambda v_: V(v_.ap.rearrange("p (h d) -> p h d", d=64), v_.tk)
            ssd0 = ssd_start[0]
            if _os.environ.get("KMARK"):
                print("MARK dt-hoist end", len(P.ops) - ssd0)
            for cc in range(4):
                if _os.environ.get("KMARK"):
                    print("MARK chunk", cc, len(P.ops) - ssd0)
                csl = slice(cc * 32, (cc + 1) * 32)
                dtv_c = V(dtv.ap[:, csl], dtv.tk)
                dtw_c = V(dtw.ap[:, csl], dtw.tk)
                dA_c = V(dA.ap[:, csl], dA.tk)
                nac_c = V(nac.ap[:, csl], nac.tk)
                eac_c = V(eac.ap[:, csl], eac.tk)
                ela_c = V(ela.ap[:, csl], ela.tk)
                for f in range(16):
                    TR(PSB(f // 8, (f % 8) * 128, (f % 8 + 1) * 128), xsT.c(f, cc * 128, (cc + 1) * 128), ident_b)
                for g in range(NG):
                    TR(PSB(2, g * 128, (g + 1) * 128), BT.c(g, cc * 128, (cc + 1) * 128), ident_b)
                for hb in range(2):
                    src = V(psb[hb][:, :].rearrange("p (h d) -> p h d", d=64), [tk_ps[hb]])
                    hs = slice(hb * 16, (hb + 1) * 16)
                    cs_ = slice(hb * 1024, (hb + 1) * 1024)
                    bc = lambda b_: V(b_.ap[:, hs].unsqueeze(2).broadcast_to([128, 16, 64]), b_.tk)
                    o3 = lambda b_: V(b_.ap[:, 0, cs_].rearrange("p (h d) -> p h d", d=64), b_.toks(0))
                    TT("dve", o3(xdt), src, bc(dtv_c), ALU.mult)
                    TT("dve", o3(xw), src, bc(dtw_c), ALU.mult)
                    TT("dve", o3(xtok), src, bc(V(Dv.ap, Dv.tk)), ALU.mult)
                CP("dve", Btok.c(0), PSB(2, 0, 512))
                for g in range(NG):
                    MM(PS(3, g * 128, (g + 1) * 128), BT.c(g, cc * 128, (cc + 1) * 128), CT.c(g, cc * 128, (cc + 1) * 128))
                CP("act", CBm.c(0), PS(3))

                def front(g):
                    e_ = g % 2
                    ab = (4, 5) if e_ == 0 else (0, 1)
                    U3 = V(cstb.ap[:, 0, C_U:C_U + 128].unsqueeze(1).broadcast_to([128, 8, 128]), cstb.toks(0))
                    dA3 = V(dA_c.ap[:, g * 8:(g + 1) * 8].unsqueeze(2).broadcast_to([128, 8, 128]), dA_c.tk)
                    TT("pool", V(dAU.ap[:, e_, :].rearrange("p (h t) -> p h t", t=128), dAU.toks(e_)), U3, dA3, ALU.mult)
                    for q in range(2):
                        MM(PS(ab[q]), ones_f, dAU.c(e_, q * 512, (q + 1) * 512), start=True, stop=False)
                        MM(PS(ab[q]), ident_b, negm.c(0), start=False, stop=True)
                    for hh in range(8):
                        h = g * 8 + hh
                        ACT(Eb.c(e_, hh * 128, (hh + 1) * 128), PS(ab[hh // 4], (hh % 4) * 128, (hh % 4 + 1) * 128), AF.Exp,
                            bias=V(nac_c.ap[:, h:h + 1], nac_c.tk))
                    E3 = V(Eb.ap[:, e_, :].rearrange("p (h t) -> p h t", t=128), Eb.toks(e_))
                    M3 = V(Mt.ap[:, e_, :].rearrange("p (h t) -> p h t", t=128), Mt.toks(e_))
                    CB3 = V(CBm.ap[:, 0, g * 128:(g + 1) * 128].unsqueeze(1).broadcast_to([128, 8, 128]), CBm.toks(0))
                    TT("dve", M3, E3, CB3, ALU.mult)

                def back(g):
                    e_ = g % 2
                    for hh in range(8):
                        h = g * 8 + hh
                        if hh == 0:
                            MM(PS(6), ident_b, xtok.c(0, g * 512, (g + 1) * 512), start=True, stop=False)
                        MM(PS(6, hh * 64, (hh + 1) * 64), Mt.c(e_, hh * 128, (hh + 1) * 128), xdt.c(0, h * 64, (h + 1) * 64),
                           start=False, stop=(hh == 7))
                    MM(PS(2), CT.c(g, cc * 128, (cc + 1) * 128), Sb.c(g))
                    eb3 = V(eac_c.ap[:, g * 8:(g + 1) * 8].unsqueeze(2).broadcast_to([128, 8, 64]), eac_c.tk)
                    TT("dve", r3(tmpa.c(e_)), r3(PS(2)), eb3, ALU.mult)
                    TT("dve", yb.c(0, g * 512, (g + 1) * 512), PS(6), tmpa.c(e_), ALU.add)

                def backS(g):
                    pbk = 3 if g % 2 == 0 else 7
                    MM(PS(pbk), Btok.c(0, g * 128, (g + 1) * 128), xw.c(0, g * 512, (g + 1) * 512))
                    el3 = V(ela_c.ap[:, g * 8:(g + 1) * 8].unsqueeze(2).broadcast_to([128, 8, 64]), ela_c.tk)
                    TT("pool", r3(Sst.c(g)), r3(Sst.c(g)), el3, ALU.mult)
                    TT("dve", Sst.c(g), Sst.c(g), PS(pbk), ALU.add)
                    CP("dve", Sb.c(g), Sst.c(g))

                front(0)
                front(1)
                back(0)
                front(2)
                back(1)
                backS(0)
                front(3)
                back(2)
                backS(1)
                back(3)
                backS(2)
                backS(3)
                TT("dve", yb.c(0), yb.c(0), zs.c(cc), ALU.mult)
                ssqy = V(ksq2.ap[:, 0, 0:1], ksq2.toks(0))
                ACT(ynb.c(0), yb.c(0), AF.Square, accum=ssqy)
                ACT(ssqy, ssqy, AF.Ln, bias=epsc, scale=1.0 / DIN)
                ACT(V(ksq2.ap[:, 0, 1:2], ksq2.toks(0)), ssqy, AF.Exp, scale=-0.5)
                ACT(ynb.c(0), yb.c(0), AF.Copy, scale=V(ksq2.ap[:, 0, 1:2], ksq2.toks(0)))
                for f in range(16):
                    TR(PSB(f // 8, (f % 8) * 128, (f % 8 + 1) * 128), ynb.c(0, f * 128, (f + 1) * 128), ident_b)
                for hb in range(2):
                    CP("dve", V(ynT.ap[:, hb * 8:(hb + 1) * 8, cc * 128:(cc + 1) * 128], ynT.toks(hb * 8, hb * 8 + 8)),
                       V(psb[hb][:, :].rearrange("p (f t) -> p f t", t=128), [tk_ps[hb]]))
            for f in range(16):
                TS("pool", ynT.c(f), ynT.c(f), vcol(l, V_SSDN + f), ALU.mult, 1.0, ALU.mult)
            if _os.environ.get("KMARK"):
                print("MARK outproj", len(P.ops) - ssd0)
            for half in range(2):
                for kg in range(2):
                    buf, tk = ws.get([wslab(w, "w_ssd_out", kg * 8, kg * 8 + 8, half * 512, 512)], extraR=[w["w_ssd_out_tk"]])
                    for k in range(8):
                        for q in range(4):
                            MM(PS(q), V(buf[:, k, q * 128:(q + 1) * 128], [tk]), ynT.c(kg * 8 + k), start=(kg == 0 and k == 0), stop=(kg == 1 and k == 7))
                buf, tk = ws.get([wslab(w, "w_in", 0, 8, 6048 + half * 512, 512)], extraR=wt)
                for q in range(4):
                    for k in range(8):
                        MM(PS(4 + q), V(buf[:, k, q * 128:(q + 1) * 128], [tk]), xn.c(k), start=(k == 0), stop=(k == 7))
                for q in range(4):
                    c = half * 4 + q
                    ACT(sg.c(q % 2), PS(4 + q), AF.Sigmoid)
                    TT("dve", sg.c(q % 2), sg.c(q % 2), PS(q), ALU.mult)
                    TT("pool", mg.c(c), mg.c(c), sg.c(q % 2), ALU.add)
            for half in range(2):
                buf, tk = ws.get([wslab(w, "w_o", 0, 8, half * 512, 512)], extraR=[w["w_o_tk"]])
                pb = half * 4
                for q in range(4):
                    for k in range(8):
                        MM(PS(pb + q), V(buf[:, k, q * 128:(q + 1) * 128], [tk]), mg.c(k), start=(k == 0), stop=(k == 7))
                for q in range(4):
                    c = half * 4 + q
                    TT("dve", hT.c(c), hT.c(c), PS(pb + q), ALU.add)

        dbg_col = [0]

        def dump(v, n):
            if dbg_out is None:
                return
            c0 = dbg_col[0]
            dbg_col[0] += n
            out_ops.append(DMA("pool", dbg_out[0:v.ap.shape[0], c0:c0 + n], v.ap, R=v.tk, semtok=Tk("dbgs")))

        for l in range(DEPTH):
            for g in range(NG):
                MEMSET("pool", Sst.c(g), 0.0)
                MEMSET("pool", Sb.c(g), 0.0)
            MEMSET("pool", halo.all(), 0.0)
            for it in range(NT):
                t0 = it * T
                if l == 0:
                    DMA("sp", hT.ap, xTv[:, :, t0:t0 + T], W=hT.toks(0, 8), semtok=hT.toks(0)[0])
                else:
                    DMA("sp", hT.ap, hbv[:, :, t0:t0 + T], R=[tk_hb[it]], W=hT.toks(0, 8), semtok=hT.toks(0)[0])
                if "f1" in PHASES:
                    ffn(l, 1)
                if "nm" in PHASES:
                    rmsnorm_to_xn(V_LNM, l)
                if "mla" in PHASES:
                    mla(l, it)
                if "ssd" in PHASES:
                    if KSSD < 999999:
                        P.limit = len(P.ops) + KSSD
                    ssd(l, it)
                    P.limit = None
                if "f2" in PHASES:
                    ffn(l, 2)
                if l == DEPTH - 1:
                    out_ops.append(DMA("pool", oTv[:, :, t0:t0 + T], hT.ap, R=hT.toks(0, 8), semtok=hT.toks(1)[0]))
                else:
                    DMA("pool", hbv[:, :, t0:t0 + T], hT.ap, R=hT.toks(0, 8), W=[tk_hb[it]], semtok=hT.toks(1)[0])
        fence = P.add("pool", None)
        ws.flush()
        kvs.flush()
        P.finalize()
        lastdma = {}
        for op_ in P.final:
            if op_.dma:
                k_ = id(op_.sem)
                if k_ not in lastdma or lastdma[k_].cnt <= op_.cnt:
                    lastdma[k_] = op_
        fence.deps = list(out_ops) + list(lastdma.values())
        if _os.environ.get("KDUMP"):
            n0 = int(_os.environ["KDUMP"])
            idx = {id(op): i for i, op in enumerate(P.final)}
            for i, op in enumerate(P.final[-n0:]):
                ds = [(idx.get(id(d), -1), d.eng, d.name, d.cnt) for d in op.deps]
                print(len(P.final) - n0 + i, op.eng, op.name, "dma" if op.dma else "", "sig" if op.sig else "", op.cnt,
                      [t.name for t in op.W][:3], "<-", ds)
        P.emit()
        build.stats = (len(P.final), P.nsem)
    return nc


def _consts():
    c = np.zeros((128, NCST), np.float32)
    c[:, C_ID:C_ID + 128] = np.eye(128, dtype=np.float32)
    c[:, C_U:C_U + 128] = np.triu(np.ones((128, 128), np.float32))
    inv = (1.0 / (np.float32(10000.0) ** (np.arange(0, 64, 2, dtype=np.float32) / np.float32(64)))).astype(np.float32)
    c[0:32, C_INV] = inv
    c[32:64, C_INV] = inv
    return c


def _pack_layer(inputs, l):
    g = lambda n: np.asarray(inputs[n][l], np.float32)
    vec = np.zeros((128, NV), np.float32)
    col = lambda v: np.ascontiguousarray(v.reshape(-1, 128).T)
    vec[:, V_LN1:V_LN1 + 8] = col(g("ln_ffn1"))
    vec[:, V_LNM:V_LNM + 8] = col(g("ln_mix"))
    vec[:, V_LN2:V_LN2 + 8] = col(g("ln_ffn2"))
    cw = g("conv_w")
    vec[:, V_CW:V_CW + 96] = cw.reshape(4, 24, 128).transpose(2, 1, 0).reshape(128, 96)
    vec[:, V_CB:V_CB + 24] = col(g("conv_b"))
    vec[:, V_QLN:V_QLN + 4] = col(g("q_lora_norm"))
    vec[:, V_KVLN:V_KVLN + 2] = col(g("kv_lora_norm"))
    qn, kn = g("q_norm"), g("k_norm")
    for (base, v) in ((V_QN, qn), (V_KN, kn)):
        vec[:, base] = v[0:128]
        vec[0:64, base + 1] = v[128:192]
        vec[0:32, base + 2] = v[160:192]
        vec[32:64, base + 2] = v[128:160]
    vec[:, V_SSDN:V_SSDN + 16] = col(g("ssd_norm"))
    rows = np.concatenate([g("dt_bias"), g("a_log"), g("d_skip")]).reshape(1, 96).astype(np.float32)
    d = {f"vecs{l}": vec, f"rows{l}": rows}
    for nm in ["ffn1_w13", "ffn1_w2", "w_in", "w_ssd_out", "w_uq", "w_ukv", "w_mla_out", "w_o", "ffn2_w13", "ffn2_w2"]:
        d[f"{nm}{l}"] = np.ascontiguousarray(g(nm))
    return d


_NC_CACHE = {}


def kernel(**inputs):
    x = np.asarray(inputs["x"], np.float32)
    B, SEQ, _ = x.shape
    DEPTH = inputs["ln_ffn1"].shape[0]
    key = (SEQ, DEPTH)
    if key not in _NC_CACHE:
        _NC_CACHE[key] = build(SEQ, DEPTH)
    nc = _NC_CACHE[key]
    shared = {"cst": _consts()}
    for l in range(DEPTH):
        shared.update(_pack_layer(inputs, l))
    in_maps = []
    for b in range(B):
        m = dict(shared)
        m["xT"] = np.ascontiguousarray(x[b].T)
        m["pos"] = np.ascontiguousarray(np.asarray(inputs["positions"][b], np.int32).reshape(1, SEQ))
        in_maps.append(m)
    res = run_bass_kernel_spmd(nc, in_maps, core_ids=list(range(B)))
    out = np.stack([np.ascontiguousarray(res.results[b]["oT"].T) for b in range(B)], axis=0)
    return out.astype(np.float32)
```
